# Optimizing a Trainium2 kernel written in Bass

```python
import math
import jax, jax.numpy as jnp
from jax import lax
import numpy as np

D_MODEL = 1024
BATCH = 4
SEQ = 4096
DEPTH = 2

D_MIX = D_MODEL
POOL_WINDOWS = (2, 4, 8, 16)
N_POOL_GROUPS = len(POOL_WINDOWS)
POOL_WIDTH = D_MIX // 4
POOL_GROUP = POOL_WIDTH // N_POOL_GROUPS
DIFF_WIDTH = D_MIX // 2
DIFF_HEAD_DIM = 64
DIFF_HEADS = DIFF_WIDTH // (2 * DIFF_HEAD_DIM)
SB_WIDTH = D_MIX - POOL_WIDTH - DIFF_WIDTH
SB_HEAD_DIM = 64
SB_HEADS = SB_WIDTH // SB_HEAD_DIM
D_IN = POOL_WIDTH + 3 * DIFF_WIDTH + 3 * SB_WIDTH
D_FF = -(-(8 * D_MODEL) // (3 * 256)) * 256
Q_BLOCK = 128
EPS = 1e-6

kernel_name = "hybrid_pool_diffattn_stickbreaking_block"


def rmsnorm(x, g):
    x32 = x.astype(jnp.float32)
    y = x32 * lax.rsqrt(jnp.mean(x32 * x32, axis=-1, keepdims=True) + EPS)
    return y.astype(x.dtype) * g


def pool_mixer(u, pool_w, pool_scale):
    B_, S, _ = u.shape
    ug = u.reshape(B_, S, N_POOL_GROUPS, POOL_GROUP)
    cs = jnp.cumsum(ug.astype(jnp.float32), axis=1)
    cs = jnp.concatenate([jnp.zeros_like(cs[:, :1]), cs], axis=1)
    t = jnp.arange(S)[:, None]
    w = jnp.array(POOL_WINDOWS, dtype=jnp.int32)[None, :]
    lo = jnp.maximum(t + 1 - w, 0)
    cnt = jnp.minimum(t + 1, w).astype(jnp.float32)[..., None]
    gidx = jnp.arange(N_POOL_GROUPS)[None, :]
    win_sum = cs[:, 1:] - cs[:, lo, gidx]
    pooled = (win_sum / cnt - ug.astype(jnp.float32)).astype(u.dtype)
    mixed = jnp.einsum('bsgc,gcd->bsgd', pooled, pool_w)
    return mixed.reshape(B_, S, POOL_WIDTH) * pool_scale


def diff_attention(q, k, v, lam, lam_init, g):
    B_, S = q.shape[0], q.shape[1]
    scale = DIFF_HEAD_DIM ** -0.5
    outs = []
    for start in range(0, S, Q_BLOCK):
        end = start + Q_BLOCK
        qb, kb, vb = q[:, start:end], k[:, :end], v[:, :end]
        s = jnp.einsum('bqhrd,bkhrd->bhrqk', qb, kb).astype(jnp.float32) * scale
        mask = jnp.arange(end)[None, :] <= (start + jnp.arange(Q_BLOCK))[:, None]
        p = jax.nn.softmax(jnp.where(mask, s, -jnp.inf), axis=-1)
        attn = p[:, :, 0] - lam * p[:, :, 1]
        outs.append(jnp.einsum('bhqk,bkhe->bqhe', attn.astype(vb.dtype), vb))
    o = jnp.concatenate(outs, axis=1)
    o = rmsnorm(o, g) * (1.0 - lam_init)
    return o.reshape(B_, S, DIFF_WIDTH)


def stick_breaking_attention(q, k, v, g):
    B_, S = q.shape[0], q.shape[1]
    scale = SB_HEAD_DIM ** -0.5
    outs = []
    for start in range(0, S, Q_BLOCK):
        end = start + Q_BLOCK
        qb, kb, vb = q[:, start:end], k[:, :end], v[:, :end]
        z = jnp.einsum('bqhd,bkhd->bhqk', qb, kb).astype(jnp.float32) * scale
        mask = jnp.arange(end)[None, :] < (start + jnp.arange(Q_BLOCK))[:, None]
        log_1mb = jnp.where(mask, jax.nn.log_sigmoid(-z), 0.0)
        suffix = lax.cumsum(log_1mb, axis=3, reverse=True) - log_1mb
        a = jnp.where(mask, jnp.exp(jax.nn.log_sigmoid(z) + suffix), 0.0)
        outs.append(jnp.einsum('bhqk,bkhd->bqhd', a.astype(vb.dtype), vb))
    o = jnp.concatenate(outs, axis=1)
    return rmsnorm(o, g).reshape(B_, S, SB_WIDTH)


def setup_inputs(seed: int = 0) -> dict:
    key = jax.random.key(seed)
    ks = jax.random.split(key, 18)
    f32 = jnp.float32
    nrm = lambda k, shape, s: (jax.random.normal(k, shape, f32) * s).astype(f32)
    return {
        "x": nrm(ks[0], (BATCH, SEQ, D_MODEL), 1.0),
        "norm1_g": 1.0 + nrm(ks[1], (DEPTH, D_MODEL), 0.02),
        "w_in": nrm(ks[2], (DEPTH, D_MODEL, D_IN), D_MODEL ** -0.5),
        "pool_w": nrm(ks[3], (DEPTH, N_POOL_GROUPS, POOL_GROUP, POOL_GROUP), POOL_GROUP ** -0.5),
        "pool_scale": 1.0 + nrm(ks[4], (DEPTH, POOL_WIDTH), 0.02),
        "lam_q1": nrm(ks[5], (DEPTH, DIFF_HEAD_DIM), 0.1),
        "lam_k1": nrm(ks[6], (DEPTH, DIFF_HEAD_DIM), 0.1),
        "lam_q2": nrm(ks[7], (DEPTH, DIFF_HEAD_DIM), 0.1),
        "lam_k2": nrm(ks[8], (DEPTH, DIFF_HEAD_DIM), 0.1),
        "diff_norm_g": 1.0 + nrm(ks[9], (DEPTH, 2 * DIFF_HEAD_DIM), 0.02),
        "sb_norm_g": 1.0 + nrm(ks[10], (DEPTH, SB_HEAD_DIM), 0.02),
        "w_out": nrm(ks[11], (DEPTH, D_MIX, D_MODEL), D_MIX ** -0.5),
        "norm2_g": 1.0 + nrm(ks[12], (DEPTH, D_MODEL), 0.02),
        "w_gate": nrm(ks[13], (DEPTH, D_MODEL, D_FF), D_MODEL ** -0.5),
        "w_up": nrm(ks[14], (DEPTH, D_MODEL, D_FF), D_MODEL ** -0.5),
        "w_down": nrm(ks[15], (DEPTH, D_FF, D_MODEL), D_FF ** -0.5),
        "final_norm_g": 1.0 + nrm(ks[16], (D_MODEL,), 0.02),
    }


def reference(x, norm1_g, w_in, pool_w, pool_scale, lam_q1, lam_k1, lam_q2, lam_k2,
              diff_norm_g, sb_norm_g, w_out, norm2_g, w_gate, w_up, w_down, final_norm_g):
    B_, S, _ = x.shape
    c0 = POOL_WIDTH
    c1 = c0 + 3 * DIFF_WIDTH
    for l in range(DEPTH):
        h = rmsnorm(x, norm1_g[l])
        p = jnp.einsum('bsd,de->bse', h, w_in[l])
        u = p[..., :c0]
        dq, dk, dv = jnp.split(p[..., c0:c1], 3, axis=-1)
        sq, sk, sv = jnp.split(p[..., c1:], 3, axis=-1)

        a_out = pool_mixer(u, pool_w[l], pool_scale[l])

        lam_init = 0.8 - 0.6 * math.exp(-0.3 * l)
        lam = (jnp.exp(jnp.sum(lam_q1[l] * lam_k1[l]).astype(jnp.float32))
               - jnp.exp(jnp.sum(lam_q2[l] * lam_k2[l]).astype(jnp.float32)) + lam_init)
        b_out = diff_attention(dq.reshape(B_, S, DIFF_HEADS, 2, DIFF_HEAD_DIM),
                               dk.reshape(B_, S, DIFF_HEADS, 2, DIFF_HEAD_DIM),
                               dv.reshape(B_, S, DIFF_HEADS, 2 * DIFF_HEAD_DIM),
                               lam, lam_init, diff_norm_g[l])

        c_out = stick_breaking_attention(sq.reshape(B_, S, SB_HEADS, SB_HEAD_DIM),
                                         sk.reshape(B_, S, SB_HEADS, SB_HEAD_DIM),
                                         sv.reshape(B_, S, SB_HEADS, SB_HEAD_DIM),
                                         sb_norm_g[l])

        mix = jnp.concatenate([a_out, b_out, c_out], axis=-1)
        x = x + jnp.einsum('bse,ed->bsd', mix, w_out[l])

        h2 = rmsnorm(x, norm2_g[l])
        gate = jnp.einsum('bsd,df->bsf', h2, w_gate[l])
        up = jnp.einsum('bsd,df->bsf', h2, w_up[l])
        x = x + jnp.einsum('bsf,fd->bsd', jax.nn.silu(gate) * up, w_down[l])
    return rmsnorm(x, final_norm_g)
```

```python
import bisect
import math

import ml_dtypes
import numpy as np

import concourse.bass as bass
import concourse.mybir as mybir
from concourse.bass_utils import run_bass_kernel_spmd

F32 = mybir.dt.float32
BF16 = mybir.dt.bfloat16
AF = mybir.ActivationFunctionType
ALU = mybir.AluOpType
AX = mybir.AxisListType

D = 1024
D_IN = 2560
D_FF = 2816
NFC = D_FF // 128
EPS = 1e-6
POOL_WINDOWS = (2, 4, 8, 16)
NEG = -30000.0
NCM = 11
(C_ONESD, C_ONES128, C_ONES64B, C_ONE, C_IDENT, C_NEGTRI, C_NEGONES,
 C_MA_D, C_MB_D, C_MA_S, C_MB_S) = range(NCM)


class Tracker:
    ENGS = ("pe", "act", "dve", "pool", "sp")

    def __init__(self):
        self.ops = []
        self.by_eng = {e: [] for e in self.ENGS}
        self.ncomp = {e: 0 for e in self.ENGS}
        self.state = {}
        self.children = {}
        self.chan_ops = {}

    def _related(self, key):
        out = []
        for n in range(1, len(key)):
            p = key[:n]
            if p in self.state:
                out.append(p)
        if key in self.state:
            out.append(key)
        out.extend(self.children.get(key, ()))
        return out

    def _register(self, key):
        if key not in self.state:
            self.state[key] = [None, []]
            for n in range(1, len(key)):
                self.children.setdefault(key[:n], set()).add(key)

    def add(self, eng, fn, reads=(), writes=(), chan=None):
        oid = len(self.ops)
        deps = {}
        for r in reads:
            for k in self._related(r):
                w = self.state[k][0]
                if w is not None:
                    deps[w] = "raw"
        for w_ in writes:
            for k in self._related(w_):
                st = self.state[k]
                if st[0] is not None:
                    deps.setdefault(st[0], "waw")
                for rd in st[1]:
                    deps.setdefault(rd, "war")
        for r in reads:
            self._register(r)
            self.state[r][1].append(oid)
        for w_ in writes:
            self._register(w_)
            self.state[w_] = [oid, []]
            for k in self.children.get(w_, ()):
                self.state[k] = [oid, []]
        deps.pop(oid, None)
        op = dict(id=oid, eng=eng, fn=fn, deps=deps, chan=chan)
        if chan is not None:
            self.chan_ops.setdefault(chan, []).append(oid)
        else:
            op["cidx"] = self.ncomp[eng]
            self.ncomp[eng] += 1
        self.ops.append(op)
        self.by_eng[eng].append(oid)
        return oid

    def emit(self, engname, eng, esem, csem):
        seen = {}
        for oid in self.by_eng[engname]:
            op = self.ops[oid]
            waits = {}
            for d, kind in op["deps"].items():
                dop = self.ops[d]
                if dop["chan"] is not None:
                    key = ("c", dop["chan"])
                    val = 16 * bisect.bisect_left(self.chan_ops[dop["chan"]], oid)
                else:
                    if dop["eng"] == engname and (engname == "pe" or kind == "war"):
                        continue
                    key = ("e", dop["eng"])
                    val = dop["cidx"] + 1
                if waits.get(key, 0) < val:
                    waits[key] = val
            for key, val in waits.items():
                if seen.get(key, 0) >= val:
                    continue
                seen[key] = val
                eng.wait_ge(csem[key[1]] if key[0] == "c" else esem[key[1]], val)
            ins = op["fn"](eng)
            if op["chan"] is not None:
                ins.then_inc(csem[op["chan"]], 16)
            else:
                ins.then_inc(esem[engname], 1)

    def final_waits(self, eng, esem, csem):
        for ch, lst in self.chan_ops.items():
            eng.wait_ge(csem[ch], 16 * len(lst))
        for e in ("pe", "act", "dve", "pool"):
            if self.ncomp[e]:
                eng.wait_ge(esem[e], self.ncomp[e])


def vec_layout(depth):
    off = {}
    n = 0
    for name, w in (("g1", depth * 8), ("g2", depth * 8), ("gf", 8), ("pscale", depth * 2),
                    ("gdiff", depth), ("gsb", depth), ("invw", 2), ("sel", 2), ("coef", 32),
                    ("lam", depth * 4 * 64)):
        off[name] = n
        n += w
    return off, n


def build(S, depth):
    NB = S // 256
    TL = NB * 128
    NG = TL // 512
    XW = 14 * TL
    KOFF, VOFF, UOFF = 0, 6 * TL, 12 * TL
    voff, NV = vec_layout(depth)
    HT = min(TL, 1024)
    NHF = TL // HT
    GH = HT // 512

    nc = bass.Bass("TRN2", target_bir_lowering=False)
    xT_d = nc.dram_tensor("xT", [D, TL], F32, kind="ExternalInput").ap()
    w_in_d = nc.dram_tensor("w_in", [depth * 20, 128, 8, 128], F32, kind="ExternalInput").ap()
    w_out_d = nc.dram_tensor("w_out", [depth * 8, 128, 8, 128], F32, kind="ExternalInput").ap()
    w_gate_d = nc.dram_tensor("w_gate", [depth * NFC, 128, 8, 128], F32, kind="ExternalInput").ap()
    w_up_d = nc.dram_tensor("w_up", [depth * NFC, 128, 8, 128], F32, kind="ExternalInput").ap()
    w_down_d = nc.dram_tensor("w_down", [depth * 8, 128, NFC, 128], F32, kind="ExternalInput").ap()
    vec_d = nc.dram_tensor("vecs", [128, NV], F32, kind="ExternalInput").ap()
    pw_d = nc.dram_tensor("poolw", [128, depth * 2 * 128], F32, kind="ExternalInput").ap()
    cm_d = nc.dram_tensor("cmats", [128, NCM * 128], BF16, kind="ExternalInput").ap()
    outT_d = nc.dram_tensor("outT", [D, TL], F32, kind="ExternalOutput").ap()
    snd_ap = [nc.dram_tensor(f"snd{i}", [128, 2 * TL], BF16).ap() for i in range(7)]
    rcv_ap = [nc.dram_tensor(f"rcv{i}", [256, 2 * TL], BF16).ap() for i in range(7)]

    T = Tracker()
    guards = []

    def sb(name, shape, dt):
        g = nc.sbuf_tensor(name, shape, dt)
        guards.append(g)
        return g.__enter__()

    def psb(name):
        g = nc.psum_tensor(name, [128, 512], F32)
        guards.append(g)
        return g.__enter__()

    NFH = NFC // 2
    RBW = 8 * TL
    xT = sb("xT_sb", [128, 8, TL], F32)
    RA = sb("RA", [128, 8 * TL], BF16)
    RB = sb("RB", [128, RBW], BF16)
    QW = max(6 * TL, 2 * NFH * 128 + (NFH - 8) * TL)
    qTr = sb("qT", [128, QW], BF16)
    qT = qTr[:, 0:6 * TL].rearrange("p (c t) -> p c t", c=6)
    uT = sb("uT", [128, 2, TL], BF16)
    wA = [sb(f"wA{i}", [128, 8, 128], BF16) for i in range(4)]
    vecs = sb("vecs_sb", [128, NV], F32)
    pwf = sb("pwf", [128, depth * 2 * 128], F32)
    pwb = sb("pwb", [128, depth * 2, 128], BF16)
    cm = sb("cm", [128, NCM, 128], BF16)
    neglam = sb("neglam", [128, depth], F32)
    gdiffs = sb("gdiffs", [128, depth], F32)
    lamtmp = sb("lamtmp", [128, 2, 64], F32)
    lams = sb("lams", [128, 4], F32)
    onecol = sb("onecol", [128, 1], F32)
    epscol = sb("epscol", [128, 1], F32)
    sqb = [sb(f"sqb{i}", [128, 512], BF16) for i in range(2)]
    tf_all = sb("tf_all", [128, 4, 512], F32)
    tf = [tf_all[:, i, :] for i in range(4)]
    Pt_all = sb("Pt_all", [128, 4, 512], BF16)
    Pt = [Pt_all[:, i, :] for i in range(4)]
    Et = [tf[2], tf[3]]
    ot = [tf[2], tf[3]]
    qz = [[sb(f"qz{r}_{i}", [128, 512], BF16) for i in range(2)] for r in range(2)]
    Lt_all = sb("Lt_all", [128, 4, 512], BF16)
    Lt = [Lt_all[:, i, :] for i in range(4)]
    Ls_all = sb("Ls_all", [128, 2, 512], BF16)
    Ls = [Ls_all[:, i, :] for i in range(2)]
    uext = sb("uext", [128, 4, 144], F32)
    s2 = sb("s2", [128, 4, 144], F32)
    s4 = sb("s4", [128, 4, 144], F32)
    g_ps = nc.psum_tensor("ps_all", [128, 8, 512], F32)
    guards.append(g_ps)
    ps_all = g_ps.__enter__()
    bank = [ps_all[:, i, :] for i in range(8)]

    hT = RA[:, :].rearrange("p (c t) -> p c t", c=8)
    mixT = hT
    v_loc = RB[:, 0:6 * TL].rearrange("p (h b e) -> p h b e", h=6, b=NB)
    kst = [RB[:, 6 * TL + i * TL: 6 * TL + (i + 1) * TL] for i in range(2)]
    KTs = [RB[:, s * 4 * TL: s * 4 * TL + 2 * TL].rearrange("p (r t) -> p r t", r=2) for s in range(2)]
    Vs = [RB[:, s * 4 * TL + 2 * TL: (s + 1) * 4 * TL].rearrange("p (r b e) -> p r b e", r=2, b=NB)
          for s in range(2)]
    uall = RB[:, 4 * TL: 8 * TL].rearrange("p (r c t) -> p r c t", r=2, c=2)
    h2T = hT

    def actT(fl):
        if fl < 8:
            return RB[:, fl * TL:(fl + 1) * TL]
        o = 2 * NFH * 128 + (fl - 8) * TL
        return qTr[:, o:o + TL]

    def actkey(fl, g):
        return ("RB", "act", fl, g) if fl < 8 else ("qT", "act", fl, g)

    wDn = [qTr[:, i * NFH * 128:(i + 1) * NFH * 128].rearrange("p (f n) -> p f n", f=NFH) for i in range(2)]

    def V(name, i=0):
        return vecs[:, voff[name] + i: voff[name] + i + 1]

    def CM(i):
        return cm[:, i, :]

    def op(eng, method, *args, reads=(), writes=(), chan=None, **kw):
        return T.add(eng, lambda e: getattr(e, method)(*args, **kw), reads=reads, writes=writes, chan=chan)

    def mm(out, lhsT, rhs, start, stop, reads, writes):
        return T.add("pe", lambda e: e.matmul(out, lhsT, rhs, start=start, stop=stop), reads=reads, writes=writes)

    rot = {"qz": 0, "P": 0, "wA": 0, "mmb": 0, "ev": 0, "sq": 0, "nb": 0, "tf": 0, "dn": 0, "ot": 0}

    def nxt(k, n):
        v = rot[k] % n
        rot[k] += 1
        return v

    op("sp", "dma_start", out=vecs[:, :], in_=vec_d[:, :], writes=[("vecs",)], chan="const")
    op("sp", "dma_start", out=pwf[:, :], in_=pw_d[:, :], writes=[("pwf",)], chan="const")
    op("sp", "dma_start", out=cm[:, :, :], in_=cm_d.rearrange("p (m n) -> p m n", m=NCM),
       writes=[("cm",)], chan="const")
    for kc in range(8):
        op("sp", "dma_start", out=xT[:, kc, :], in_=xT_d[kc * 128:(kc + 1) * 128, :],
           writes=[("xT", kc)], chan="xin")
    op("dve", "memset", onecol[:, :], 1.0, writes=[("onecol",)])
    op("dve", "memset", epscol[:, :], EPS, writes=[("epscol",)])
    for r in range(2):
        for i in range(2):
            op("dve", "memset", qz[r][i][:, :], 0.0, writes=[("qz", i)])
    op("dve", "tensor_copy", out=pwb[:, :, :], in_=pwf[:, :].rearrange("p (m n) -> p m n", n=128),
       reads=[("pwf",)], writes=[("pwb",)])
    for l in range(depth):
        lam_init = 0.8 - 0.6 * math.exp(-0.3 * l)
        lo = voff["lam"] + l * 256
        lv = vecs[:, lo:lo + 256].rearrange("p (i d) -> p i d", i=4)
        op("dve", "tensor_tensor", out=lamtmp[:, 0, :], in0=lv[:, 0, :], in1=lv[:, 1, :], op=ALU.mult,
           reads=[("vecs",)], writes=[("lamtmp",)])
        op("dve", "tensor_tensor", out=lamtmp[:, 1, :], in0=lv[:, 2, :], in1=lv[:, 3, :], op=ALU.mult,
           reads=[("vecs",)], writes=[("lamtmp",)])
        op("dve", "reduce_sum", out=lams[:, 0:2], in_=lamtmp[:, :, :], axis=AX.X,
           reads=[("lamtmp",)], writes=[("lams",)])
        op("act", "activation", out=lams[:, 2:4], in_=lams[:, 0:2], func=AF.Exp,
           reads=[("lams",)], writes=[("lams",)])
        op("dve", "tensor_tensor", out=neglam[:, l:l + 1], in0=lams[:, 3:4], in1=lams[:, 2:3], op=ALU.subtract,
           reads=[("lams",)], writes=[("neglam", l)])
        op("dve", "tensor_scalar", out=neglam[:, l:l + 1], in0=neglam[:, l:l + 1], scalar1=-lam_init,
           scalar2=None, op0=ALU.add, reads=[("neglam", l)], writes=[("neglam", l)])
        op("dve", "tensor_scalar", out=gdiffs[:, l:l + 1], in0=V("gdiff", l), scalar1=1.0 - lam_init,
           scalar2=None, op0=ALU.mult, reads=[("vecs",)], writes=[("gdiffs", l)])

    def rmsnorm_stats(src_fn, src_keys, g, nbank, ones_idx=C_ONESD, tslot=None):
        b = bank[nbank]
        for kc in range(8):
            s = nxt("sq", 2)
            op("act", "activation", out=sqb[s][:, :], in_=src_fn(kc), func=AF.Square,
               reads=[src_keys(kc)], writes=[("sqb", s)])
            mm(b[:, :], CM(ones_idx), sqb[s][:, :], kc == 0, kc == 7,
               reads=[("sqb", s), ("cm",)], writes=[("bank", nbank)])
        t = nxt("tf", 4) if tslot is None else tslot
        op("act", "activation", out=tf[t][:, :], in_=b[:, :], func=AF.Ln, bias=epscol[:, 0:1], scale=1.0,
           reads=[("bank", nbank), ("epscol",)], writes=[("tf", t)])
        op("act", "activation", out=tf[t][:, :], in_=tf[t][:, :], func=AF.Exp, scale=-0.5,
           reads=[("tf", t)], writes=[("tf", t)])
        return t

    def load_wA(src_ap):
        s = nxt("wA", 4)
        op("pool", "dma_start", out=wA[s][:, :, :], in_=src_ap, writes=[("wA", s)], chan=f"wA{s}")
        flush_ag(3)
        return s

    pending_ag = []

    def flush_ag(age):
        while pending_ag and rot["wA"] - pending_ag[0][1] >= age:
            i = pending_ag.pop(0)[0]
            T.add("pool", lambda e, i=i: e.collective_compute(
                "AllGather", ALU.bypass, replica_groups=[[0, 1], [2, 3], [4, 5], [6, 7]],
                ins=[snd_ap[i].opt()], outs=[rcv_ap[i].opt()]), reads=[("snd", i)], writes=[("rcv", i)])

    def evac(out_ap, in_ap, reads, writes, scale=None):
        e = nxt("ev", 2)
        if e == 0:
            if scale is None:
                op("act", "copy", out=out_ap, in_=in_ap, reads=reads, writes=writes)
            else:
                op("act", "mul", out=out_ap, in_=in_ap, mul=scale, reads=reads, writes=writes)
        else:
            if scale is None:
                op("dve", "tensor_copy", out=out_ap, in_=in_ap, reads=reads, writes=writes)
            else:
                op("dve", "tensor_scalar", out=out_ap, in0=in_ap, scalar1=scale, scalar2=None,
                   op0=ALU.mult, reads=reads, writes=writes)

    for l in range(depth):
        for g in range(NG):
            gs = slice(g * 512, (g + 1) * 512)
            nb = 6 + nxt("nb", 2)
            t = rmsnorm_stats(lambda kc: xT[:, kc, gs], lambda kc: ("xT", kc, g), g, nb)
            for kc in range(8):
                op("dve", "scalar_tensor_tensor", out=hT[:, kc, gs], in0=xT[:, kc, gs],
                   scalar=V("g1", l * 8 + kc), in1=tf[t][:, :], op0=ALU.mult, op1=ALU.mult,
                   reads=[("xT", kc, g), ("tf", t), ("vecs",)], writes=[("RA", kc, g)])
        def allgather(i):
            pending_ag.append((i, rot["wA"]))

        def fm_chunk(cc, dest):
            s = load_wA(w_in_d[l * 20 + cc])
            for g in range(NG):
                gs = slice(g * 512, (g + 1) * 512)
                b = nxt("mmb", 6)
                for kc in range(8):
                    mm(bank[b][:, :], wA[s][:, kc, :], hT[:, kc, gs], kc == 0, kc == 7,
                       reads=[("RA", kc, g), ("wA", s)], writes=[("bank", b)])
                dest(g, gs, b)

        for hcv in range(6):
            ccv = 10 + hcv if hcv < 4 else 18 + (hcv - 4)
            s = load_wA(w_in_d[l * 20 + ccv])
            for g in range(NG):
                b = nxt("mmb", 6)
                for ib in range(4):
                    i = g * 4 + ib
                    for kc in range(8):
                        mm(bank[b][:, ib * 128:(ib + 1) * 128], hT[:, kc, i * 128:(i + 1) * 128], wA[s][:, kc, :],
                           kc == 0, kc == 7, reads=[("RA", kc, g), ("wA", s)], writes=[("bank", b)])
                evac(v_loc[:, hcv, g * 4:(g + 1) * 4, :], bank[b][:, :].rearrange("p (b e) -> p b e", e=128),
                     reads=[("bank", b)], writes=[("RB", "v", hcv, g)])
            op("sp", "dma_start", out=snd_ap[hcv][:, TL:2 * TL], in_=RB[:, hcv * TL:(hcv + 1) * TL],
               reads=[("RB", "v", hcv)], writes=[("snd", hcv, "v")], chan="snd")
            cc = 6 + hcv if hcv < 4 else 16 + (hcv - 4)
            ks = hcv % 2
            fm_chunk(cc, lambda g, gs, b: evac(kst[ks][:, gs], bank[b][:, :], reads=[("bank", b)],
                                               writes=[("RB", "k", ks, g)]))
            op("sp", "dma_start", out=snd_ap[hcv][:, 0:TL], in_=kst[ks],
               reads=[("RB", "k", ks)], writes=[("snd", hcv, "k")], chan="snd")
            allgather(hcv)
        for cc in range(2):
            fm_chunk(cc, lambda g, gs, b: evac(uT[:, cc, gs], bank[b][:, :], reads=[("bank", b)],
                                               writes=[("uT", cc, g)]))
        op("sp", "dma_start", out=snd_ap[6][:, :], in_=uT[:, :, :].rearrange("p c t -> p (c t)"),
           reads=[("uT",)], writes=[("snd", 6)], chan="snd")
        allgather(6)
        for qc in range(6):
            cc = 2 + qc if qc < 4 else 14 + (qc - 4)
            fm_chunk(cc, lambda g, gs, b: evac(qT[:, qc, gs], bank[b][:, :], reads=[("bank", b)],
                                               writes=[("qT", qc, g)], scale=0.125))

        flush_ag(0)
        op("dve", "memset", onecol[:, :], 1.0, reads=[("RA",), ("RB",), ("qT",)], writes=[("RA",), ("RB",), ("qT",), ("onecol",)])

        def load_kv(hc):
            s = hc % 2
            for r in range(2):
                op("sp", "dma_start", out=KTs[s][:, r, :], in_=rcv_ap[hc][r * 128:(r + 1) * 128, 0:TL],
                   reads=[("rcv", hc)], writes=[("RB", "kv", s)], chan=f"kv{s}")
                op("sp", "dma_start", out=Vs[s][:, r, :, :],
                   in_=rcv_ap[hc][r * 128:(r + 1) * 128, TL:2 * TL].rearrange("p (b e) -> p b e", e=128),
                   reads=[("rcv", hc)], writes=[("RB", "kv", s)], chan=f"kv{s}")

        load_kv(0)
        for r in range(2):
            op("sp", "dma_start", out=uall[:, r, :, :],
               in_=rcv_ap[6][r * 128:(r + 1) * 128, :].rearrange("p (c t) -> p c t", c=2),
               reads=[("rcv", 6)], writes=[("RB", "kv", 1)], chan="kv1")
        uT4 = uT[:, :, :].rearrange("p c (b t) -> p c b t", t=128)
        ua4 = [uall[:, r, :, :].rearrange("p c (b t) -> p c b t", t=128) for r in range(2)]
        for c in range(2):
            for g in range(NG):
                gs = slice(g * 512, (g + 1) * 512)
                b0 = g * 4
                op("dve", "tensor_copy", out=uext[:, :, 16:144], in_=uT4[:, c, b0:b0 + 4, :],
                   reads=[("uT", c, g)], writes=[("uext",)])
                op("dve", "tensor_scalar", out=uext[:, :, 0:16], in0=ua4[0][:, c, b0:b0 + 4, 112:128],
                   scalar1=V("sel", 0), scalar2=None, op0=ALU.mult,
                   reads=[("RB", "kv", 1), ("vecs",)], writes=[("uext",)])
                lo_b = 1 if b0 == 0 else 0
                op("dve", "scalar_tensor_tensor", out=uext[:, lo_b:4, 0:16],
                   in0=ua4[1][:, c, b0 + lo_b - 1:b0 + 3, 112:128], scalar=V("sel", 1),
                   in1=uext[:, lo_b:4, 0:16], op0=ALU.mult, op1=ALU.add,
                   reads=[("RB", "kv", 1), ("vecs",), ("uext",)], writes=[("uext",)])
                op("dve", "tensor_tensor", out=s2[:, :, 1:144], in0=uext[:, :, 1:144], in1=uext[:, :, 0:143],
                   op=ALU.add, reads=[("uext",)], writes=[("s2",)])
                op("dve", "tensor_tensor", out=s4[:, :, 3:144], in0=s2[:, :, 3:144], in1=s2[:, :, 1:142],
                   op=ALU.add, reads=[("s2",)], writes=[("s4",)])
                if c == 1:
                    op("dve", "tensor_tensor", out=s2[:, :, 7:144], in0=s4[:, :, 7:144], in1=s4[:, :, 3:140],
                       op=ALU.add, reads=[("s4",), ("s2",)], writes=[("s2",)])
                    op("dve", "tensor_tensor", out=s4[:, :, 15:144], in0=s2[:, :, 15:144], in1=s2[:, :, 7:136],
                       op=ALU.add, reads=[("s2",), ("s4",)], writes=[("s4",)])
                p = nxt("P", 4)
                pl = Pt[p][:, :].rearrange("p (b t) -> p b t", t=128)
                for (lo, hi, src_t) in ((0, 64, s2), (64, 128, s4)):
                    op("dve", "scalar_tensor_tensor", out=pl[lo:hi, :, :], in0=src_t[lo:hi, :, 16:144],
                       scalar=vecs[lo:hi, voff["invw"] + c:voff["invw"] + c + 1], in1=uext[lo:hi, :, 16:144],
                       op0=ALU.mult, op1=ALU.subtract, reads=[("s2",), ("s4",), ("uext",), ("vecs",)],
                       writes=[("Pt", p)])
                    if g == 0:
                        op("dve", "tensor_tensor", out=tf[0][lo:hi, 0:16], in0=src_t[lo:hi, 0, 16:32],
                           in1=vecs[lo:hi, voff["coef"] + c * 16:voff["coef"] + c * 16 + 16], op=ALU.mult,
                           reads=[("s2",), ("s4",), ("vecs",)], writes=[("tf", 0)])
                        op("dve", "tensor_tensor", out=pl[lo:hi, 0, 0:16], in0=tf[0][lo:hi, 0:16],
                           in1=uext[lo:hi, 0, 16:32], op=ALU.subtract, reads=[("tf", 0), ("uext",)],
                           writes=[("Pt", p)])
                b = nxt("mmb", 4)
                mm(bank[b][:, :], pwb[:, l * 2 + c, :], Pt[p][:, :], True, True,
                   reads=[("Pt", p), ("pwb",)], writes=[("bank", b)])
                op("dve", "tensor_scalar", out=mixT[:, c, gs], in0=bank[b][:, :], scalar1=V("pscale", l * 2 + c),
                   scalar2=None, op0=ALU.mult, reads=[("bank", b), ("vecs",)], writes=[("RA", c, g)])

        iters = [(hc, g) for hc in range(6) for g in range(NG)]
        deferred = []

        def prep_qz(hc, g):
            qs = nxt("qz", 2)
            gs = slice(g * 512, (g + 1) * 512)
            op("dve", "tensor_copy", out=qz[0][qs][0:64, :], in_=qT[0:64, hc, gs],
               reads=[("qT", hc, g)], writes=[("qz", qs)])
            op("dve", "tensor_copy", out=qz[1][qs][64:128, :], in_=qT[64:128, hc, gs],
               reads=[("qT", hc, g)], writes=[("qz", qs)])
            return qs

        def run_deferred():
            while deferred:
                deferred.pop(0)()

        qs_next = prep_qz(*iters[0])
        for it, (hc, g) in enumerate(iters):
            s = hc % 2
            if g == 0 and hc + 1 < 6:
                load_kv(hc + 1)
            is_diff = hc < 4
            mA = CM(C_MA_D if is_diff else C_MA_S)
            mB = CM(C_MB_D if is_diff else C_MB_S)
            mch = 2 + hc
            nkb = 8 * g + 8
            gs = slice(g * 512, (g + 1) * 512)
            qs = qs_next
            if it + 1 < len(iters):
                qs_next = prep_qz(*iters[it + 1])

            def c0_of(kb):
                return (max(4 * g, kb // 2) - 4 * g) * 128

            def qk_pair(sl, kb, stop=True):
                c0 = c0_of(kb)
                rk, lb = kb % 2, kb // 2
                im = kb // 2
                has_mask = im >= 4 * g
                for r in range(2):
                    bk = 2 * sl + r
                    mm(bank[bk][:, c0:512], KTs[s][:, rk, lb * 128:(lb + 1) * 128], qz[r][qs][:, c0:512],
                       True, stop and not has_mask, reads=[("RB", "kv", s), ("qz", qs)], writes=[("bank", bk)])
                    if has_mask:
                        mc = (im - 4 * g) * 128
                        mm(bank[bk][:, mc:mc + 128], CM(C_IDENT), (mA if kb % 2 == 0 else mB), False, stop,
                           reads=[("cm",)], writes=[("bank", bk)])

            def vblk(kb):
                return Vs[s][:, kb % 2, kb // 2, :]

            def bk2(sl):
                return [("bank", 2 * sl), ("bank", 2 * sl + 1)]

            def pt2(sl):
                return [("Pt", 2 * sl), ("Pt", 2 * sl + 1)]

            if is_diff:
                kbs = list(range(nkb))
                n = len(kbs)
                for j in range(n + 1):
                    if j < n:
                        kb = kbs[j]
                        c0 = c0_of(kb)
                        sl = j % 2
                        qk_pair(sl, kb)
                        op("act", "activation", out=Pt_all[:, 2 * sl:2 * sl + 2, c0:512],
                           in_=ps_all[:, 2 * sl:2 * sl + 2, c0:512], func=AF.Exp, reads=bk2(sl), writes=pt2(sl))
                    if j >= 1:
                        kb = kbs[j - 1]
                        c0 = c0_of(kb)
                        sl = (j - 1) % 2
                        last = kb == nkb - 1
                        for r in range(2):
                            p = 2 * sl + r
                            mm(bank[4 + r][:, c0:512], vblk(kb), Pt[p][:, c0:512], kb == 0, last,
                               reads=[("Pt", p), ("RB", "kv", s)], writes=[("bank", 4 + r)])
                            mm(bank[6 + r][:, c0:512], CM(C_ONE), Pt[p][:, c0:512], kb == 0, last,
                               reads=[("Pt", p), ("cm",)], writes=[("bank", 6 + r)])
                    if j == 2:
                        run_deferred()
                op("dve", "reciprocal", out=tf[0][:, :], in_=bank[6][:, :], reads=[("bank", 6)], writes=[("tf", 0)])
                op("dve", "tensor_tensor", out=tf[0][:, :], in0=bank[4][:, :], in1=tf[0][:, :], op=ALU.mult,
                   reads=[("bank", 4), ("tf", 0)], writes=[("tf", 0)])
                op("dve", "reciprocal", out=tf[1][:, :], in_=bank[7][:, :], reads=[("bank", 7)], writes=[("tf", 1)])
                op("dve", "tensor_tensor", out=tf[1][:, :], in0=bank[5][:, :], in1=tf[1][:, :], op=ALU.mult,
                   reads=[("bank", 5), ("tf", 1)], writes=[("tf", 1)])
                ones_i, gvec, gkey = C_ONES128, gdiffs[:, l:l + 1], ("gdiffs", l)
                combine = True
            else:
                kbs = list(range(nkb - 1, -1, -1))
                n = len(kbs)
                op("dve", "memset", Ls_all[:, :, :], 0.0, writes=[("Ls", 0), ("Ls", 1)])
                for j in range(n + 2):
                    if j < n:
                        kb = kbs[j]
                        c0 = c0_of(kb)
                        sl = j % 2
                        qk_pair(sl, kb, stop=False)
                        op("act", "activation", out=tf_all[:, 2:4, c0:512], in_=ps_all[:, 2 * sl:2 * sl + 2, c0:512],
                           func=AF.Exp, reads=bk2(sl), writes=[("tf", 2), ("tf", 3)])
                        op("act", "activation", out=Lt_all[:, 2 * sl:2 * sl + 2, c0:512], in_=tf_all[:, 2:4, c0:512],
                           func=AF.Ln, bias=onecol[:, 0:1], scale=1.0,
                           reads=[("tf", 2), ("tf", 3), ("onecol",)], writes=[("Lt", 2 * sl), ("Lt", 2 * sl + 1)])
                    if 1 <= j <= n:
                        kb = kbs[j - 1]
                        c0 = c0_of(kb)
                        sl = (j - 1) % 2
                        first = kb == nkb - 1
                        for r in range(2):
                            z = 2 * sl + r
                            mm(bank[z][:, c0:512], CM(C_NEGTRI), Lt[z][:, c0:512], False, first,
                               reads=[("Lt", z), ("cm",)], writes=[("bank", z)])
                        if not first:
                            for r in range(2):
                                z = 2 * sl + r
                                mm(bank[z][:, c0:512], CM(C_NEGONES), Ls[r][:, c0:512], False, True,
                                   reads=[("Ls", r), ("cm",)], writes=[("bank", z)])
                        op("dve", "tensor_tensor", out=Ls_all[:, :, c0:512], in0=Ls_all[:, :, c0:512],
                           in1=Lt_all[:, 2 * sl:2 * sl + 2, c0:512], op=ALU.add,
                           reads=[("Ls", 0), ("Ls", 1), ("Lt", 2 * sl), ("Lt", 2 * sl + 1)],
                           writes=[("Ls", 0), ("Ls", 1)])
                        op("act", "activation", out=Pt_all[:, 2 * sl:2 * sl + 2, c0:512],
                           in_=ps_all[:, 2 * sl:2 * sl + 2, c0:512], func=AF.Exp, reads=bk2(sl), writes=pt2(sl))
                    if j >= 2:
                        kb = kbs[j - 2]
                        c0 = c0_of(kb)
                        sl = (j - 2) % 2
                        for r in range(2):
                            p = 2 * sl + r
                            mm(bank[4 + r][:, c0:512], vblk(kb), Pt[p][:, c0:512], kb == nkb - 1, kb == 0,
                               reads=[("Pt", p), ("RB", "kv", s)], writes=[("bank", 4 + r)])
                    if j == 3:
                        run_deferred()
                op("dve", "tensor_copy", out=tf[0][0:64, :], in_=bank[4][0:64, :], reads=[("bank", 4)],
                   writes=[("tf", 0, 0)])
                op("dve", "tensor_copy", out=tf[0][64:128, :], in_=bank[5][64:128, :], reads=[("bank", 5)],
                   writes=[("tf", 0, 1)])
                ones_i, gvec, gkey = C_ONES64B, V("gsb", l), ("vecs",)
                combine = False
            run_deferred()

            def part_b(combine=combine, ones_i=ones_i, gvec=gvec, gkey=gkey, mch=mch, gs=gs, g=g):
                if combine:
                    op("dve", "scalar_tensor_tensor", out=tf[0][:, :], in0=tf[1][:, :], scalar=neglam[:, l:l + 1],
                       in1=tf[0][:, :], op0=ALU.mult, op1=ALU.add,
                       reads=[("tf", 0), ("tf", 1), ("neglam", l)], writes=[("tf", 0)])
                op("act", "activation", out=sqb[0][:, :], in_=tf[0][:, :], func=AF.Square,
                   reads=[("tf", 0)], writes=[("sqb", 0)])
                mm(bank[0][:, :], CM(ones_i), sqb[0][:, :], True, True, reads=[("sqb", 0), ("cm",)],
                   writes=[("bank", 0)])
                op("act", "activation", out=tf[1][:, :], in_=bank[0][:, :], func=AF.Ln, bias=epscol[:, 0:1],
                   scale=1.0, reads=[("bank", 0), ("epscol",)], writes=[("tf", 1)])
                op("act", "activation", out=tf[1][:, :], in_=tf[1][:, :], func=AF.Exp, scale=-0.5,
                   reads=[("tf", 1)], writes=[("tf", 1)])
                op("dve", "scalar_tensor_tensor", out=mixT[:, mch, gs], in0=tf[0][:, :], scalar=gvec,
                   in1=tf[1][:, :], op0=ALU.mult, op1=ALU.mult, reads=[("tf", 0), ("tf", 1), gkey],
                   writes=[("RA", mch, g)])

            deferred.append(part_b)
        run_deferred()

        for dc in range(8):
            s = load_wA(w_out_d[l * 8 + dc])
            for g in range(NG):
                gs = slice(g * 512, (g + 1) * 512)
                b = nxt("mmb", 6)
                for ec in range(8):
                    mm(bank[b][:, :], wA[s][:, ec, :], mixT[:, ec, gs], ec == 0, ec == 7,
                       reads=[("RA", ec, g), ("wA", s)], writes=[("bank", b)])
                op("dve", "tensor_tensor", out=xT[:, dc, gs], in0=bank[b][:, :], in1=xT[:, dc, gs], op=ALU.add,
                   reads=[("bank", b), ("xT", dc, g)], writes=[("xT", dc, g)])

        op("dve", "memset", onecol[:, :], 1.0, reads=[("RA",), ("RB",), ("qT",)], writes=[("RA",), ("RB",), ("qT",), ("onecol",)])
        for g in range(NG):
            gs = slice(g * 512, (g + 1) * 512)
            nb = 6 + nxt("nb", 2)
            t = rmsnorm_stats(lambda kc: xT[:, kc, gs], lambda kc: ("xT", kc, g), g, nb)
            for kc in range(8):
                op("dve", "scalar_tensor_tensor", out=h2T[:, kc, gs], in0=xT[:, kc, gs],
                   scalar=V("g2", l * 8 + kc), in1=tf[t][:, :], op0=ALU.mult, op1=ALU.mult,
                   reads=[("xT", kc, g), ("tf", t), ("vecs",)], writes=[("RA", "h2", kc, g)])
        for fh in range(2):
            for fl in range(NFH):
                fc = fh * NFH + fl
                sg = load_wA(w_gate_d[l * NFC + fc])
                su = load_wA(w_up_d[l * NFC + fc])
                for g in range(NG):
                    gs = slice(g * 512, (g + 1) * 512)
                    bg = nxt("mmb", 6)
                    for kc in range(8):
                        mm(bank[bg][:, :], wA[sg][:, kc, :], h2T[:, kc, gs], kc == 0, kc == 7,
                           reads=[("RA", "h2", kc, g), ("wA", sg)], writes=[("bank", bg)])
                    bu = nxt("mmb", 6)
                    for kc in range(8):
                        mm(bank[bu][:, :], wA[su][:, kc, :], h2T[:, kc, gs], kc == 0, kc == 7,
                           reads=[("RA", "h2", kc, g), ("wA", su)], writes=[("bank", bu)])
                    t = nxt("tf", 4)
                    op("act", "activation", out=tf[t][:, :], in_=bank[bg][:, :], func=AF.Silu,
                       reads=[("bank", bg)], writes=[("tf", t)])
                    op("dve", "tensor_tensor", out=actT(fl)[:, gs], in0=bank[bu][:, :], in1=tf[t][:, :], op=ALU.mult,
                       reads=[("bank", bu), ("tf", t)], writes=[actkey(fl, g)])
            for dc in range(8):
                sd = nxt("dn", 2)
                op("pool", "dma_start", out=wDn[sd][:, :, :],
                   in_=w_down_d[l * 8 + dc][:, fh * NFH:(fh + 1) * NFH, :],
                   writes=[("qT", "dn", sd)], chan=f"wD{sd}")
                for g in range(NG):
                    gs = slice(g * 512, (g + 1) * 512)
                    b = nxt("mmb", 6)
                    for fl in range(NFH):
                        mm(bank[b][:, :], wDn[sd][:, fl, :], actT(fl)[:, gs], fl == 0, fl == NFH - 1,
                           reads=[actkey(fl, g), ("qT", "dn", sd)], writes=[("bank", b)])
                    op("dve", "tensor_tensor", out=xT[:, dc, gs], in0=bank[b][:, :], in1=xT[:, dc, gs], op=ALU.add,
                       reads=[("bank", b), ("xT", dc, g)], writes=[("xT", dc, g)])
        op("dve", "memset", onecol[:, :], 1.0, reads=[("RA",), ("RB",), ("qT",)], writes=[("RA",), ("RB",), ("qT",), ("onecol",)])

    for g in range(NG):
        gs = slice(g * 512, (g + 1) * 512)
        nb = 6 + nxt("nb", 2)
        t = rmsnorm_stats(lambda kc: xT[:, kc, gs], lambda kc: ("xT", kc, g), g, nb, tslot=g % 2)
        for kc in range(8):
            o = nxt("ot", 2)
            op("dve", "scalar_tensor_tensor", out=ot[o][:, :], in0=xT[:, kc, gs],
               scalar=V("gf", kc), in1=tf[t][:, :], op0=ALU.mult, op1=ALU.mult,
               reads=[("xT", kc, g), ("tf", t), ("vecs",)], writes=[("tf", 2 + o)])
            op("sp", "dma_start", out=outT_d[kc * 128:(kc + 1) * 128, gs], in_=ot[o][:, :],
               reads=[("tf", 2 + o)], writes=[("out", kc, g)], chan=f"out{o}")

    print("sbuf bytes remaining per partition:", nc.sbuf_bytes_remaining)
    chans = list(T.chan_ops.keys())
    sem_guards = []

    def mksem(name):
        g_ = nc.semaphore(name)
        sem_guards.append(g_)
        return g_.__enter__()

    esem = {e: mksem(f"e_{e}") for e in ("pe", "act", "dve", "pool")}
    csem = {c: mksem(f"c_{c}") for c in chans}
    with nc.Block() as block:
        @block.sync
        def _(e):
            T.emit("sp", e, esem, csem)
            T.final_waits(e, esem, csem)

        @block.tensor
        def _(e):
            T.emit("pe", e, esem, csem)

        @block.scalar
        def _(e):
            T.emit("act", e, esem, csem)

        @block.vector
        def _(e):
            T.emit("dve", e, esem, csem)

        @block.gpsimd
        def _(e):
            T.emit("pool", e, esem, csem)
    for g_ in reversed(sem_guards):
        g_.__exit__(None, None, None)
    for g_ in reversed(guards):
        g_.__exit__(None, None, None)
    return nc, T


def _const_mats(j):
    bf = ml_dtypes.bfloat16
    k = np.arange(128)[:, None]
    q = np.arange(128)[None, :]
    m = np.zeros((NCM, 128, 128), np.float32)
    m[C_ONESD] = 1.0 / D
    m[C_ONES128] = 1.0 / 128
    m[C_ONES64B][:64, :64] = 1.0 / 64
    m[C_ONES64B][64:, 64:] = 1.0 / 64
    m[C_ONE] = 1.0
    m[C_IDENT] = np.eye(128)
    m[C_NEGTRI] = np.where(k >= q, -1.0, 0.0)
    m[C_NEGONES] = -1.0
    diag_d = np.where(k <= q, 0.0, NEG)
    diag_s = np.where(k < q, 0.0, NEG)
    full = np.zeros((128, 128))
    none = np.full((128, 128), NEG)
    if j == 0:
        m[C_MA_D], m[C_MB_D], m[C_MA_S], m[C_MB_S] = diag_d, none, diag_s, none
    else:
        m[C_MA_D], m[C_MB_D], m[C_MA_S], m[C_MB_S] = full, diag_d, full, diag_s
    return np.ascontiguousarray(m.transpose(1, 0, 2).reshape(128, NCM * 128)).astype(bf)


def _pack_vecs(depth, j, norm1_g, norm2_g, final_norm_g, pool_scale, diff_norm_g, sb_norm_g,
               lam_q1, lam_k1, lam_q2, lam_k2):
    voff, NV = vec_layout(depth)
    v = np.zeros((128, NV), np.float32)
    p = np.arange(128)
    for l in range(depth):
        for kc in range(8):
            v[:, voff["g1"] + l * 8 + kc] = norm1_g[l, kc * 128:(kc + 1) * 128]
            v[:, voff["g2"] + l * 8 + kc] = norm2_g[l, kc * 128:(kc + 1) * 128]
        for c in range(2):
            v[:, voff["pscale"] + l * 2 + c] = pool_scale[l, c * 128:(c + 1) * 128]
        v[:, voff["gdiff"] + l] = diff_norm_g[l]
        v[:, voff["gsb"] + l] = sb_norm_g[l][p % 64]
        for i, a in enumerate((lam_q1, lam_k1, lam_q2, lam_k2)):
            o = voff["lam"] + (l * 4 + i) * 64
            v[:, o:o + 64] = a[l][None, :]
    for kc in range(8):
        v[:, voff["gf"] + kc] = final_norm_g[kc * 128:(kc + 1) * 128]
    for c in range(2):
        w = np.array([POOL_WINDOWS[2 * c + pp // 64] for pp in p], np.float32)
        v[:, voff["invw"] + c] = 1.0 / w
        for t in range(16):
            cnt = np.minimum(t + 1, w) if j == 0 else w
            v[:, voff["coef"] + c * 16 + t] = 1.0 / cnt
    v[:, voff["sel"] + 0] = 0.0 if j == 0 else 1.0
    v[:, voff["sel"] + 1] = 1.0 if j == 0 else 0.0
    return v


def _pack_poolw(depth, pool_w):
    m = np.zeros((128, depth * 2, 128), np.float32)
    for l in range(depth):
        for c in range(2):
            m[0:64, l * 2 + c, 0:64] = pool_w[l, 2 * c]
            m[64:128, l * 2 + c, 64:128] = pool_w[l, 2 * c + 1]
    return np.ascontiguousarray(m.reshape(128, depth * 2 * 128))


_CACHE = {}


def run(x, norm1_g, w_in, pool_w, pool_scale, lam_q1, lam_k1, lam_q2, lam_k2, diff_norm_g, sb_norm_g,
        w_out, norm2_g, w_gate, w_up, w_down, final_norm_g, trace=False):
    f = lambda a: np.ascontiguousarray(np.asarray(a, dtype=np.float32))
    x = f(x)
    B, S, _ = x.shape
    depth = int(np.asarray(w_in).shape[0])
    assert B == 4 and S % 1024 == 0
    NB = S // 256
    TL = NB * 128
    key = (S, depth)
    if key not in _CACHE:
        _CACHE[key] = build(S, depth)[0]
    nc = _CACHE[key]
    def tile_w(w, kchunks, nchunks):
        w = f(w).reshape(depth, kchunks, 128, nchunks, 128).transpose(0, 3, 2, 1, 4)
        return np.ascontiguousarray(w).reshape(depth * nchunks, 128, kchunks, 128)

    w_in, w_out = tile_w(w_in, 8, 20), tile_w(w_out, 8, 8)
    w_gate, w_up, w_down = tile_w(w_gate, 8, NFC), tile_w(w_up, 8, NFC), tile_w(w_down, NFC, 8)
    pw = _pack_poolw(depth, f(pool_w))
    in_maps = []
    for c in range(8):
        b, j = c // 2, c % 2
        xs = x[b].reshape(NB, 2, 128, D)[:, j].reshape(TL, D)
        in_maps.append({
            "xT": np.ascontiguousarray(xs.T),
            "w_in": w_in, "w_out": w_out, "w_gate": w_gate, "w_up": w_up, "w_down": w_down,
            "vecs": _pack_vecs(depth, j, f(norm1_g), f(norm2_g), f(final_norm_g), f(pool_scale),
                               f(diff_norm_g), f(sb_norm_g), f(lam_q1), f(lam_k1), f(lam_q2), f(lam_k2)),
            "poolw": pw,
            "cmats": _const_mats(j),
        })
    res = run_bass_kernel_spmd(nc, in_maps, core_ids=list(range(8)), **({"trace": True} if trace else {}))
    out = np.empty((B, S, D), np.float32)
    for c in range(8):
        b, j = c // 2, c % 2
        o = np.asarray(res.results[c]["outT"], dtype=np.float32).T.reshape(NB, 128, D)
        out[b].reshape(NB, 2, 128, D)[:, j] = o
    return out, res


def kernel(**inputs):
    out, _ = run(**inputs)
    return out
```

```python
import bisect
import math

import ml_dtypes
import numpy as np

import concourse.bass as bass
import concourse.mybir as mybir
from concourse.bass_utils import run_bass_kernel_spmd

F32 = mybir.dt.float32
BF16 = mybir.dt.bfloat16
AF = mybir.ActivationFunctionType
ALU = mybir.AluOpType
AX = mybir.AxisListType

D = 1024
D_IN = 2560
D_FF = 2816
NFC = D_FF // 128
EPS = 1e-6
POOL_WINDOWS = (2, 4, 8, 16)
NEG = -30000.0
NCM = 11
(C_ONESD, C_ONES128, C_ONES64B, C_ONE, C_IDENT, C_NEGTRI, C_NEGONES,
 C_MA_D, C_MB_D, C_MA_S, C_MB_S) = range(NCM)


class Tracker:
    ENGS = ("pe", "act", "dve", "pool", "sp")

    def __init__(self):
        self.ops = []
        self.by_eng = {e: [] for e in self.ENGS}
        self.ncomp = {e: 0 for e in self.ENGS}
        self.state = {}
        self.children = {}
        self.chan_ops = {}

    def _related(self, key):
        out = []
        for n in range(1, len(key)):
            p = key[:n]
            if p in self.state:
                out.append(p)
        if key in self.state:
            out.append(key)
        out.extend(self.children.get(key, ()))
        return out

    def _register(self, key):
        if key not in self.state:
            self.state[key] = [None, []]
            for n in range(1, len(key)):
                self.children.setdefault(key[:n], set()).add(key)

    def add(self, eng, fn, reads=(), writes=(), chan=None):
        oid = len(self.ops)
        deps = {}
        for r in reads:
            for k in self._related(r):
                w = self.state[k][0]
                if w is not None:
                    deps[w] = "raw"
        for w_ in writes:
            for k in self._related(w_):
                st = self.state[k]
                if st[0] is not None:
                    deps.setdefault(st[0], "waw")
                for rd in st[1]:
                    deps.setdefault(rd, "war")
        for r in reads:
            self._register(r)
            self.state[r][1].append(oid)
        for w_ in writes:
            self._register(w_)
            self.state[w_] = [oid, []]
            for k in self.children.get(w_, ()):
                self.state[k] = [oid, []]
        deps.pop(oid, None)
        op = dict(id=oid, eng=eng, fn=fn, deps=deps, chan=chan)
        if chan is not None:
            self.chan_ops.setdefault(chan, []).append(oid)
        else:
            op["cidx"] = self.ncomp[eng]
            self.ncomp[eng] += 1
        self.ops.append(op)
        self.by_eng[eng].append(oid)
        return oid

    def emit(self, engname, eng, esem, csem):
        seen = {}
        for oid in self.by_eng[engname]:
            op = self.ops[oid]
            waits = {}
            for d, kind in op["deps"].items():
                dop = self.ops[d]
                if dop["chan"] is not None:
                    key = ("c", dop["chan"])
                    val = 16 * bisect.bisect_left(self.chan_ops[dop["chan"]], oid)
                else:
                    if dop["eng"] == engname and (engname == "pe" or kind == "war"):
                        continue
                    key = ("e", dop["eng"])
                    val = dop["cidx"] + 1
                if waits.get(key, 0) < val:
                    waits[key] = val
            for key, val in waits.items():
                if seen.get(key, 0) >= val:
                    continue
                seen[key] = val
                eng.wait_ge(csem[key[1]] if key[0] == "c" else esem[key[1]], val)
            ins = op["fn"](eng)
            if op["chan"] is not None:
                ins.then_inc(csem[op["chan"]], 16)
            else:
                ins.then_inc(esem[engname], 1)

    def final_waits(self, eng, esem, csem):
        for ch, lst in self.chan_ops.items():
            eng.wait_ge(csem[ch], 16 * len(lst))
        for e in ("pe", "act", "dve", "pool"):
            if self.ncomp[e]:
                eng.wait_ge(esem[e], self.ncomp[e])


def vec_layout(depth):
    off = {}
    n = 0
    for name, w in (("g1", depth * 8), ("g2", depth * 8), ("gf", 8), ("pscale", depth * 2),
                    ("gdiff", depth), ("gsb", depth), ("invw", 2), ("sel", 2), ("coef", 32),
                    ("lam", depth * 4 * 64)):
        off[name] = n
        n += w
    return off, n


def build(S, depth):
    NB = S // 256
    TL = NB * 128
    NG = TL // 512
    XW = 14 * TL
    KOFF, VOFF, UOFF = 0, 6 * TL, 12 * TL
    voff, NV = vec_layout(depth)
    HT = min(TL, 1024)
    NHF = TL // HT
    GH = HT // 512

    nc = bass.Bass("TRN2", target_bir_lowering=False)
    xT_d = nc.dram_tensor("xT", [D, TL], F32, kind="ExternalInput").ap()
    w_in_d = nc.dram_tensor("w_in", [depth * 20, 128, 8, 128], F32, kind="ExternalInput").ap()
    w_out_d = nc.dram_tensor("w_out", [depth * 8, 128, 8, 128], F32, kind="ExternalInput").ap()
    w_gate_d = nc.dram_tensor("w_gate", [depth * NFC, 128, 8, 128], F32, kind="ExternalInput").ap()
    w_up_d = nc.dram_tensor("w_up", [depth * NFC, 128, 8, 128], F32, kind="ExternalInput").ap()
    w_down_d = nc.dram_tensor("w_down", [depth * 8, 128, NFC, 128], F32, kind="ExternalInput").ap()
    vec_d = nc.dram_tensor("vecs", [128, NV], F32, kind="ExternalInput").ap()
    pw_d = nc.dram_tensor("poolw", [128, depth * 2 * 128], F32, kind="ExternalInput").ap()
    cm_d = nc.dram_tensor("cmats", [128, NCM * 128], BF16, kind="ExternalInput").ap()
    outT_d = nc.dram_tensor("outT", [D, TL], F32, kind="ExternalOutput").ap()
    snd_ap = [nc.dram_tensor(f"snd{i}", [128, 2 * TL], BF16).ap() for i in range(7)]
    rcv_ap = [nc.dram_tensor(f"rcv{i}", [256, 2 * TL], BF16).ap() for i in range(7)]

    T = Tracker()
    guards = []

    def sb(name, shape, dt):
        g = nc.sbuf_tensor(name, shape, dt)
        guards.append(g)
        return g.__enter__()

    def psb(name):
        g = nc.psum_tensor(name, [128, 512], F32)
        guards.append(g)
        return g.__enter__()

    NFH = NFC // 2
    RBW = 8 * TL
    xT = sb("xT_sb", [128, 8, TL], F32)
    RA = sb("RA", [128, 8 * TL], BF16)
    RB = sb("RB", [128, RBW], BF16)
    QW = max(6 * TL, 2 * NFH * 128 + (NFH - 8) * TL)
    qTr = sb("qT", [128, QW], BF16)
    qT = qTr[:, 0:6 * TL].rearrange("p (c t) -> p c t", c=6)
    uT = sb("uT", [128, 2, TL], BF16)
    wA = [sb(f"wA{i}", [128, 8, 128], BF16) for i in range(4)]
    vecs = sb("vecs_sb", [128, NV], F32)
    pwf = sb("pwf", [128, depth * 2 * 128], F32)
    pwb = sb("pwb", [128, depth * 2, 128], BF16)
    cm = sb("cm", [128, NCM, 128], BF16)
    neglam = sb("neglam", [128, depth], F32)
    gdiffs = sb("gdiffs", [128, depth], F32)
    lamtmp = sb("lamtmp", [128, 2, 64], F32)
    lams = sb("lams", [128, 4], F32)
    onecol = sb("onecol", [128, 1], F32)
    epscol = sb("epscol", [128, 1], F32)
    sqb = [sb(f"sqb{i}", [128, 512], BF16) for i in range(2)]
    tf_all = sb("tf_all", [128, 4, 512], F32)
    tf = [tf_all[:, i, :] for i in range(4)]
    Pt_all = sb("Pt_all", [128, 4, 512], BF16)
    Pt = [Pt_all[:, i, :] for i in range(4)]
    Et = [tf[2], tf[3]]
    ot = [tf[2], tf[3]]
    qz = [[sb(f"qz{r}_{i}", [128, 512], BF16) for i in range(2)] for r in range(2)]
    Lt_all = sb("Lt_all", [128, 4, 512], BF16)
    Lt = [Lt_all[:, i, :] for i in range(4)]
    Ls_all = sb("Ls_all", [128, 2, 512], BF16)
    Ls = [Ls_all[:, i, :] for i in range(2)]
    tf_flat = tf_all[:, :, :].rearrange("p a b -> p (a b)")
    uext = tf_flat[:, 0:576].rearrange("p (b t) -> p b t", t=144)
    s2 = tf_flat[:, 576:1152].rearrange("p (b t) -> p b t", t=144)
    s4 = tf_flat[:, 1152:1728].rearrange("p (b t) -> p b t", t=144)
    fxt = tf_flat[:, 1728:1744]
    acc = sb("acc", [128, 512], F32)
    onesf = sb("onesf", [128, 128], F32)
    scr = sb("scr", [128, 1], F32)
    g_ps = nc.psum_tensor("ps_all", [128, 8, 512], F32)
    guards.append(g_ps)
    ps_all = g_ps.__enter__()
    bank = [ps_all[:, i, :] for i in range(8)]

    hT = RA[:, :].rearrange("p (c t) -> p c t", c=8)
    mixT = hT
    v_loc = RB[:, 0:6 * TL].rearrange("p (h b e) -> p h b e", h=6, b=NB)
    kst = [RB[:, 6 * TL + i * TL: 6 * TL + (i + 1) * TL] for i in range(2)]
    KTs = [RB[:, s * 4 * TL: s * 4 * TL + 2 * TL].rearrange("p (r t) -> p r t", r=2) for s in range(2)]
    Vs = [RB[:, s * 4 * TL + 2 * TL: (s + 1) * 4 * TL].rearrange("p (r b e) -> p r b e", r=2, b=NB)
          for s in range(2)]
    uall = RB[:, 4 * TL: 8 * TL].rearrange("p (r c t) -> p r c t", r=2, c=2)
    h2T = hT

    def actT(fl):
        if fl < 8:
            return RB[:, fl * TL:(fl + 1) * TL]
        o = 2 * NFH * 128 + (fl - 8) * TL
        return qTr[:, o:o + TL]

    def actkey(fl, g):
        return ("RB", "act", fl, g) if fl < 8 else ("qT", "act", fl, g)

    wDn = [qTr[:, i * NFH * 128:(i + 1) * NFH * 128].rearrange("p (f n) -> p f n", f=NFH) for i in range(2)]

    def V(name, i=0):
        return vecs[:, voff[name] + i: voff[name] + i + 1]

    def CM(i):
        return cm[:, i, :]

    def op(eng, method, *args, reads=(), writes=(), chan=None, **kw):
        return T.add(eng, lambda e: getattr(e, method)(*args, **kw), reads=reads, writes=writes, chan=chan)

    def mm(out, lhsT, rhs, start, stop, reads, writes):
        return T.add("pe", lambda e: e.matmul(out, lhsT, rhs, start=start, stop=stop), reads=reads, writes=writes)

    rot = {"qz": 0, "P": 0, "wA": 0, "mmb": 0, "ev": 0, "sq": 0, "nb": 0, "tf": 0, "dn": 0, "ot": 0}

    def nxt(k, n):
        v = rot[k] % n
        rot[k] += 1
        return v

    op("sp", "dma_start", out=vecs[:, :], in_=vec_d[:, :], writes=[("vecs",)], chan="const")
    op("sp", "dma_start", out=pwf[:, :], in_=pw_d[:, :], writes=[("pwf",)], chan="const")
    op("sp", "dma_start", out=cm[:, :, :], in_=cm_d.rearrange("p (m n) -> p m n", m=NCM),
       writes=[("cm",)], chan="const")
    for kc in range(8):
        op("sp", "dma_start", out=xT[:, kc, :], in_=xT_d[kc * 128:(kc + 1) * 128, :],
           writes=[("xT", kc)], chan="xin")
    op("dve", "memset", onecol[:, :], 1.0, writes=[("onecol",)])
    op("dve", "memset", epscol[:, :], EPS, writes=[("epscol",)])
    op("dve", "memset", onesf[:, :], 1.0, writes=[("onesf",)])
    for r in range(2):
        for i in range(2):
            op("dve", "memset", qz[r][i][:, :], 0.0, writes=[("qz", i)])
    op("dve", "tensor_copy", out=pwb[:, :, :], in_=pwf[:, :].rearrange("p (m n) -> p m n", n=128),
       reads=[("pwf",)], writes=[("pwb",)])
    for l in range(depth):
        lam_init = 0.8 - 0.6 * math.exp(-0.3 * l)
        lo = voff["lam"] + l * 256
        lv = vecs[:, lo:lo + 256].rearrange("p (i d) -> p i d", i=4)
        op("dve", "tensor_tensor", out=lamtmp[:, 0, :], in0=lv[:, 0, :], in1=lv[:, 1, :], op=ALU.mult,
           reads=[("vecs",)], writes=[("lamtmp",)])
        op("dve", "tensor_tensor", out=lamtmp[:, 1, :], in0=lv[:, 2, :], in1=lv[:, 3, :], op=ALU.mult,
           reads=[("vecs",)], writes=[("lamtmp",)])
        op("dve", "reduce_sum", out=lams[:, 0:2], in_=lamtmp[:, :, :], axis=AX.X,
           reads=[("lamtmp",)], writes=[("lams",)])
        op("act", "activation", out=lams[:, 2:4], in_=lams[:, 0:2], func=AF.Exp,
           reads=[("lams",)], writes=[("lams",)])
        op("dve", "tensor_tensor", out=neglam[:, l:l + 1], in0=lams[:, 3:4], in1=lams[:, 2:3], op=ALU.subtract,
           reads=[("lams",)], writes=[("neglam", l)])
        op("dve", "tensor_scalar", out=neglam[:, l:l + 1], in0=neglam[:, l:l + 1], scalar1=-lam_init,
           scalar2=None, op0=ALU.add, reads=[("neglam", l)], writes=[("neglam", l)])
        op("dve", "tensor_scalar", out=gdiffs[:, l:l + 1], in0=V("gdiff", l), scalar1=1.0 - lam_init,
           scalar2=None, op0=ALU.mult, reads=[("vecs",)], writes=[("gdiffs", l)])

    def rmsnorm_stats(src_fn, src_keys, g, nbank, ones_idx=C_ONESD, tslot=None):
        b = bank[nbank]
        for kc in range(8):
            s = nxt("sq", 2)
            op("act", "activation", out=sqb[s][:, :], in_=src_fn(kc), func=AF.Square,
               reads=[src_keys(kc)], writes=[("sqb", s)])
            mm(b[:, :], CM(ones_idx), sqb[s][:, :], kc == 0, kc == 7,
               reads=[("sqb", s), ("cm",)], writes=[("bank", nbank)])
        t = nxt("tf", 4) if tslot is None else tslot
        op("act", "activation", out=tf[t][:, :], in_=b[:, :], func=AF.Ln, bias=epscol[:, 0:1], scale=1.0,
           reads=[("bank", nbank), ("epscol",)], writes=[("tf", t)])
        op("act", "activation", out=tf[t][:, :], in_=tf[t][:, :], func=AF.Exp, scale=-0.5,
           reads=[("tf", t)], writes=[("tf", t)])
        return t

    def load_wA(src_ap):
        s = nxt("wA", 4)
        op("pool", "dma_start", out=wA[s][:, :, :], in_=src_ap, writes=[("wA", s)], chan=f"wA{s}")
        flush_ag(3)
        return s

    pending_ag = []

    def flush_ag(age):
        while pending_ag and rot["wA"] - pending_ag[0][1] >= age:
            i = pending_ag.pop(0)[0]
            T.add("pool", lambda e, i=i: e.collective_compute(
                "AllGather", ALU.bypass, replica_groups=[[0, 1], [2, 3], [4, 5], [6, 7]],
                ins=[snd_ap[i].opt()], outs=[rcv_ap[i].opt()]), reads=[("snd", i)], writes=[("rcv", i)])

    def evac(out_ap, in_ap, reads, writes, scale=None):
        e = nxt("ev", 2)
        if e == 0:
            if scale is None:
                op("act", "copy", out=out_ap, in_=in_ap, reads=reads, writes=writes)
            else:
                op("act", "mul", out=out_ap, in_=in_ap, mul=scale, reads=reads, writes=writes)
        else:
            if scale is None:
                op("dve", "tensor_copy", out=out_ap, in_=in_ap, reads=reads, writes=writes)
            else:
                op("dve", "tensor_scalar", out=out_ap, in0=in_ap, scalar1=scale, scalar2=None,
                   op0=ALU.mult, reads=reads, writes=writes)

    for l in range(depth):
        for g in range(NG):
            gs = slice(g * 512, (g + 1) * 512)
            nb = 6 + nxt("nb", 2)
            t = rmsnorm_stats(lambda kc: xT[:, kc, gs], lambda kc: ("xT", kc, g), g, nb)
            for kc in range(8):
                op("dve", "scalar_tensor_tensor", out=hT[:, kc, gs], in0=xT[:, kc, gs],
                   scalar=V("g1", l * 8 + kc), in1=tf[t][:, :], op0=ALU.mult, op1=ALU.mult,
                   reads=[("xT", kc, g), ("tf", t), ("vecs",)], writes=[("RA", kc, g)])
        def allgather(i):
            pending_ag.append((i, rot["wA"]))

        def fm_chunk(cc, dest):
            s = load_wA(w_in_d[l * 20 + cc])
            for g in range(NG):
                gs = slice(g * 512, (g + 1) * 512)
                b = nxt("mmb", 6)
                for kc in range(8):
                    mm(bank[b][:, :], wA[s][:, kc, :], hT[:, kc, gs], kc == 0, kc == 7,
                       reads=[("RA", kc, g), ("wA", s)], writes=[("bank", b)])
                dest(g, gs, b)

        for hcv in range(6):
            ccv = 10 + hcv if hcv < 4 else 18 + (hcv - 4)
            s = load_wA(w_in_d[l * 20 + ccv])
            for g in range(NG):
                b = nxt("mmb", 6)
                for ib in range(4):
                    i = g * 4 + ib
                    for kc in range(8):
                        mm(bank[b][:, ib * 128:(ib + 1) * 128], hT[:, kc, i * 128:(i + 1) * 128], wA[s][:, kc, :],
                           kc == 0, kc == 7, reads=[("RA", kc, g), ("wA", s)], writes=[("bank", b)])
                evac(v_loc[:, hcv, g * 4:(g + 1) * 4, :], bank[b][:, :].rearrange("p (b e) -> p b e", e=128),
                     reads=[("bank", b)], writes=[("RB", "v", hcv, g)])
            op("sp", "dma_start", out=snd_ap[hcv][:, TL:2 * TL], in_=RB[:, hcv * TL:(hcv + 1) * TL],
               reads=[("RB", "v", hcv)], writes=[("snd", hcv, "v")], chan="snd")
            cc = 6 + hcv if hcv < 4 else 16 + (hcv - 4)
            ks = hcv % 2
            fm_chunk(cc, lambda g, gs, b: evac(kst[ks][:, gs], bank[b][:, :], reads=[("bank", b)],
                                               writes=[("RB", "k", ks, g)]))
            op("sp", "dma_start", out=snd_ap[hcv][:, 0:TL], in_=kst[ks],
               reads=[("RB", "k", ks)], writes=[("snd", hcv, "k")], chan="snd")
            allgather(hcv)
        for cc in range(2):
            fm_chunk(cc, lambda g, gs, b: evac(uT[:, cc, gs], bank[b][:, :], reads=[("bank", b)],
                                               writes=[("uT", cc, g)]))
        op("sp", "dma_start", out=snd_ap[6][:, :], in_=uT[:, :, :].rearrange("p c t -> p (c t)"),
           reads=[("uT",)], writes=[("snd", 6)], chan="snd")
        allgather(6)
        for qc in range(6):
            cc = 2 + qc if qc < 4 else 14 + (qc - 4)
            fm_chunk(cc, lambda g, gs, b: evac(qT[:, qc, gs], bank[b][:, :], reads=[("bank", b)],
                                               writes=[("qT", qc, g)], scale=0.125))

        flush_ag(0)
        op("dve", "memset", onecol[:, :], 1.0, reads=[("RA",), ("RB",), ("qT",)], writes=[("RA",), ("RB",), ("qT",), ("onecol",)])

        def load_kv(hc):
            s = hc % 2
            for r in range(2):
                op("sp", "dma_start", out=KTs[s][:, r, :], in_=rcv_ap[hc][r * 128:(r + 1) * 128, 0:TL],
                   reads=[("rcv", hc)], writes=[("RB", "kv", s)], chan=f"kv{s}")
                op("sp", "dma_start", out=Vs[s][:, r, :, :],
                   in_=rcv_ap[hc][r * 128:(r + 1) * 128, TL:2 * TL].rearrange("p (b e) -> p b e", e=128),
                   reads=[("rcv", hc)], writes=[("RB", "kv", s)], chan=f"kv{s}")

        load_kv(0)
        for r in range(2):
            op("sp", "dma_start", out=uall[:, r, :, :],
               in_=rcv_ap[6][r * 128:(r + 1) * 128, :].rearrange("p (c t) -> p c t", c=2),
               reads=[("rcv", 6)], writes=[("RB", "kv", 1)], chan="kv1")
        op("dve", "memset", scr[:, :], 0.0, reads=[("tf",)], writes=[("tf",), ("scr",)])
        uT4 = uT[:, :, :].rearrange("p c (b t) -> p c b t", t=128)
        ua4 = [uall[:, r, :, :].rearrange("p c (b t) -> p c b t", t=128) for r in range(2)]
        for c in range(2):
            for g in range(NG):
                gs = slice(g * 512, (g + 1) * 512)
                b0 = g * 4
                op("dve", "tensor_copy", out=uext[:, :, 16:144], in_=uT4[:, c, b0:b0 + 4, :],
                   reads=[("uT", c, g)], writes=[("tf", "u")])
                op("dve", "tensor_scalar", out=uext[:, :, 0:16], in0=ua4[0][:, c, b0:b0 + 4, 112:128],
                   scalar1=V("sel", 0), scalar2=None, op0=ALU.mult,
                   reads=[("RB", "kv", 1), ("vecs",)], writes=[("tf", "u")])
                lo_b = 1 if b0 == 0 else 0
                op("dve", "scalar_tensor_tensor", out=uext[:, lo_b:4, 0:16],
                   in0=ua4[1][:, c, b0 + lo_b - 1:b0 + 3, 112:128], scalar=V("sel", 1),
                   in1=uext[:, lo_b:4, 0:16], op0=ALU.mult, op1=ALU.add,
                   reads=[("RB", "kv", 1), ("vecs",), ("tf", "u")], writes=[("tf", "u")])
                op("dve", "tensor_tensor", out=s2[:, :, 1:144], in0=uext[:, :, 1:144], in1=uext[:, :, 0:143],
                   op=ALU.add, reads=[("tf", "u")], writes=[("tf", "s2")])
                op("dve", "tensor_tensor", out=s4[:, :, 3:144], in0=s2[:, :, 3:144], in1=s2[:, :, 1:142],
                   op=ALU.add, reads=[("tf", "s2")], writes=[("tf", "s4")])
                if c == 1:
                    op("dve", "tensor_tensor", out=s2[:, :, 7:144], in0=s4[:, :, 7:144], in1=s4[:, :, 3:140],
                       op=ALU.add, reads=[("tf", "s4"), ("tf", "s2")], writes=[("tf", "s2")])
                    op("dve", "tensor_tensor", out=s4[:, :, 15:144], in0=s2[:, :, 15:144], in1=s2[:, :, 7:136],
                       op=ALU.add, reads=[("tf", "s2"), ("tf", "s4")], writes=[("tf", "s4")])
                p = nxt("P", 4)
                pl = Pt[p][:, :].rearrange("p (b t) -> p b t", t=128)
                for (lo, hi, src_t) in ((0, 64, s2), (64, 128, s4)):
                    op("dve", "scalar_tensor_tensor", out=pl[lo:hi, :, :], in0=src_t[lo:hi, :, 16:144],
                       scalar=vecs[lo:hi, voff["invw"] + c:voff["invw"] + c + 1], in1=uext[lo:hi, :, 16:144],
                       op0=ALU.mult, op1=ALU.subtract, reads=[("tf", "s2"), ("tf", "s4"), ("tf", "u"), ("vecs",)],
                       writes=[("Pt", p)])
                    if g == 0:
                        op("dve", "tensor_tensor", out=fxt[lo:hi, :], in0=src_t[lo:hi, 0, 16:32],
                           in1=vecs[lo:hi, voff["coef"] + c * 16:voff["coef"] + c * 16 + 16], op=ALU.mult,
                           reads=[("tf", "s2"), ("tf", "s4"), ("vecs",)], writes=[("tf", "fx")])
                        op("dve", "tensor_tensor", out=pl[lo:hi, 0, 0:16], in0=fxt[lo:hi, :],
                           in1=uext[lo:hi, 0, 16:32], op=ALU.subtract, reads=[("tf", "fx"), ("tf", "u")],
                           writes=[("Pt", p)])
                b = nxt("mmb", 4)
                mm(bank[b][:, :], pwb[:, l * 2 + c, :], Pt[p][:, :], True, True,
                   reads=[("Pt", p), ("pwb",)], writes=[("bank", b)])
                op("dve", "tensor_scalar", out=mixT[:, c, gs], in0=bank[b][:, :], scalar1=V("pscale", l * 2 + c),
                   scalar2=None, op0=ALU.mult, reads=[("bank", b), ("vecs",)], writes=[("RA", c, g)])

        op("dve", "memset", scr[:, :], 0.0, reads=[("tf",)], writes=[("tf",), ("scr",)])
        iters = [(hc, g) for hc in range(6) for g in range(NG)]
        deferred = []

        def prep_qz(hc, g):
            qs = nxt("qz", 2)
            gs = slice(g * 512, (g + 1) * 512)
            op("dve", "tensor_copy", out=qz[0][qs][0:64, :], in_=qT[0:64, hc, gs],
               reads=[("qT", hc, g)], writes=[("qz", qs)])
            op("dve", "tensor_copy", out=qz[1][qs][64:128, :], in_=qT[64:128, hc, gs],
               reads=[("qT", hc, g)], writes=[("qz", qs)])
            return qs

        def run_deferred():
            while deferred:
                deferred.pop(0)()

        qs_next = prep_qz(*iters[0])
        for it, (hc, g) in enumerate(iters):
            s = hc % 2
            if g == 0 and hc + 1 < 6:
                load_kv(hc + 1)
            is_diff = hc < 4
            mA = CM(C_MA_D if is_diff else C_MA_S)
            mB = CM(C_MB_D if is_diff else C_MB_S)
            mch = 2 + hc
            nkb = 8 * g + 8
            gs = slice(g * 512, (g + 1) * 512)
            qs = qs_next
            if it + 1 < len(iters):
                qs_next = prep_qz(*iters[it + 1])

            def c0_of(kb):
                return (max(4 * g, kb // 2) - 4 * g) * 128

            def qk_pair(sl, kb, stop=True):
                c0 = c0_of(kb)
                rk, lb = kb % 2, kb // 2
                im = kb // 2
                has_mask = im >= 4 * g
                for r in range(2):
                    bk = 2 * sl + r
                    mm(bank[bk][:, c0:512], KTs[s][:, rk, lb * 128:(lb + 1) * 128], qz[r][qs][:, c0:512],
                       True, stop and not has_mask, reads=[("RB", "kv", s), ("qz", qs)], writes=[("bank", bk)])
                    if has_mask:
                        mc = (im - 4 * g) * 128
                        mm(bank[bk][:, mc:mc + 128], CM(C_IDENT), (mA if kb % 2 == 0 else mB), False, stop,
                           reads=[("cm",)], writes=[("bank", bk)])

            def vblk(kb):
                return Vs[s][:, kb % 2, kb // 2, :]

            def bk2(sl):
                return [("bank", 2 * sl), ("bank", 2 * sl + 1)]

            def pt2(sl):
                return [("Pt", 2 * sl), ("Pt", 2 * sl + 1)]

            if is_diff:
                kbs = list(range(nkb))
                n = len(kbs)
                for j in range(n + 1):
                    if j < n:
                        kb = kbs[j]
                        c0 = c0_of(kb)
                        sl = j % 2
                        qk_pair(sl, kb)
                        op("act", "activation", out=Pt_all[:, 2 * sl:2 * sl + 2, c0:512],
                           in_=ps_all[:, 2 * sl:2 * sl + 2, c0:512], func=AF.Exp, reads=bk2(sl), writes=pt2(sl))
                    if j >= 1:
                        kb = kbs[j - 1]
                        c0 = c0_of(kb)
                        sl = (j - 1) % 2
                        last = kb == nkb - 1
                        for r in range(2):
                            p = 2 * sl + r
                            mm(bank[4 + r][:, c0:512], vblk(kb), Pt[p][:, c0:512], kb == 0, last,
                               reads=[("Pt", p), ("RB", "kv", s)], writes=[("bank", 4 + r)])
                        mm(bank[7][:, c0:512], CM(C_ONE), Pt[2 * sl + 1][:, c0:512], kb == 0, last,
                           reads=[("Pt", 2 * sl + 1), ("cm",)], writes=[("bank", 7)])
                        if kb == 0:
                            op("dve", "tensor_copy", out=acc[:, :], in_=Pt[2 * sl][:, :],
                               reads=[("Pt", 2 * sl)], writes=[("acc",)])
                        else:
                            op("dve", "tensor_tensor", out=acc[:, c0:512], in0=acc[:, c0:512],
                               in1=Pt[2 * sl][:, c0:512], op=ALU.add, reads=[("acc",), ("Pt", 2 * sl)],
                               writes=[("acc",)])
                        if last:
                            mm(bank[6][:, :], onesf[:, :], acc[:, :], True, True,
                               reads=[("acc",), ("onesf",)], writes=[("bank", 6)])
                    if j == 2:
                        run_deferred()
                op("dve", "tensor_copy", out=tf[0][:, :], in_=bank[4][:, :], reads=[("bank", 4)], writes=[("tf", 0)])
                op("act", "copy", out=tf[1][:, :], in_=bank[5][:, :], reads=[("bank", 5)], writes=[("tf", 1)])
                op("dve", "tensor_copy", out=tf[2][:, :], in_=bank[6][:, :], reads=[("bank", 6)], writes=[("tf", 2)])
                op("act", "copy", out=tf[3][:, :], in_=bank[7][:, :], reads=[("bank", 7)], writes=[("tf", 3)])
                ones_i, gvec, gkey = C_ONES128, gdiffs[:, l:l + 1], ("gdiffs", l)
                combine = True
            else:
                kbs = list(range(nkb - 1, -1, -1))
                n = len(kbs)
                run_deferred()
                op("dve", "memset", Ls_all[:, :, :], 0.0, writes=[("Ls", 0), ("Ls", 1)])
                for j in range(n + 2):
                    if j < n:
                        kb = kbs[j]
                        c0 = c0_of(kb)
                        sl = j % 3
                        lsl = j % 2
                        qk_pair(sl, kb, stop=False)
                        op("act", "activation", out=tf_all[:, 2:4, c0:512], in_=ps_all[:, 2 * sl:2 * sl + 2, c0:512],
                           func=AF.Exp, reads=bk2(sl), writes=[("tf", 2), ("tf", 3)])
                        op("act", "activation", out=Lt_all[:, 2 * lsl:2 * lsl + 2, c0:512],
                           in_=tf_all[:, 2:4, c0:512], func=AF.Ln, bias=onecol[:, 0:1], scale=1.0,
                           reads=[("tf", 2), ("tf", 3), ("onecol",)], writes=[("Lt", 2 * lsl), ("Lt", 2 * lsl + 1)])
                    if 1 <= j <= n:
                        kb = kbs[j - 1]
                        c0 = c0_of(kb)
                        sl = (j - 1) % 3
                        lsl = (j - 1) % 2
                        first = kb == nkb - 1
                        for r in range(2):
                            mm(bank[2 * sl + r][:, c0:512], CM(C_NEGTRI), Lt[2 * lsl + r][:, c0:512], False, first,
                               reads=[("Lt", 2 * lsl + r), ("cm",)], writes=[("bank", 2 * sl + r)])
                        if not first:
                            for r in range(2):
                                mm(bank[2 * sl + r][:, c0:512], CM(C_NEGONES), Ls[r][:, c0:512], False, True,
                                   reads=[("Ls", r), ("cm",)], writes=[("bank", 2 * sl + r)])
                        op("dve", "tensor_tensor", out=Ls_all[:, :, c0:512], in0=Ls_all[:, :, c0:512],
                           in1=Lt_all[:, 2 * lsl:2 * lsl + 2, c0:512], op=ALU.add,
                           reads=[("Ls", 0), ("Ls", 1), ("Lt", 2 * lsl), ("Lt", 2 * lsl + 1)],
                           writes=[("Ls", 0), ("Ls", 1)])
                        op("act", "activation", out=Pt_all[:, 2 * lsl:2 * lsl + 2, c0:512],
                           in_=ps_all[:, 2 * sl:2 * sl + 2, c0:512], func=AF.Exp, reads=bk2(sl), writes=pt2(lsl))
                    if j >= 2:
                        kb = kbs[j - 2]
                        c0 = c0_of(kb)
                        lsl = (j - 2) % 2
                        for r in range(2):
                            p = 2 * lsl + r
                            mm(bank[6 + r][:, c0:512], vblk(kb), Pt[p][:, c0:512], kb == nkb - 1, kb == 0,
                               reads=[("Pt", p), ("RB", "kv", s)], writes=[("bank", 6 + r)])
                    if j == 3:
                        run_deferred()
                op("dve", "tensor_copy", out=tf[0][0:64, :], in_=bank[6][0:64, :], reads=[("bank", 6)],
                   writes=[("tf", 0, 0)])
                op("dve", "tensor_copy", out=tf[0][64:128, :], in_=bank[7][64:128, :], reads=[("bank", 7)],
                   writes=[("tf", 0, 1)])
                ones_i, gvec, gkey = C_ONES64B, V("gsb", l), ("vecs",)
                combine = False
            run_deferred()

            def part_b(combine=combine, ones_i=ones_i, gvec=gvec, gkey=gkey, mch=mch, gs=gs, g=g):
                if combine:
                    op("dve", "reciprocal", out=tf[2][:, :], in_=tf[2][:, :], reads=[("tf", 2)], writes=[("tf", 2)])
                    op("dve", "tensor_tensor", out=tf[0][:, :], in0=tf[0][:, :], in1=tf[2][:, :], op=ALU.mult,
                       reads=[("tf", 0), ("tf", 2)], writes=[("tf", 0)])
                    op("dve", "reciprocal", out=tf[3][:, :], in_=tf[3][:, :], reads=[("tf", 3)], writes=[("tf", 3)])
                    op("dve", "tensor_tensor", out=tf[1][:, :], in0=tf[1][:, :], in1=tf[3][:, :], op=ALU.mult,
                       reads=[("tf", 1), ("tf", 3)], writes=[("tf", 1)])
                    op("dve", "scalar_tensor_tensor", out=tf[0][:, :], in0=tf[1][:, :], scalar=neglam[:, l:l + 1],
                       in1=tf[0][:, :], op0=ALU.mult, op1=ALU.add,
                       reads=[("tf", 0), ("tf", 1), ("neglam", l)], writes=[("tf", 0)])
                op("act", "activation", out=sqb[0][:, :], in_=tf[0][:, :], func=AF.Square,
                   reads=[("tf", 0)], writes=[("sqb", 0)])
                mm(bank[0][:, :], CM(ones_i), sqb[0][:, :], True, True, reads=[("sqb", 0), ("cm",)],
                   writes=[("bank", 0)])
                op("act", "activation", out=tf[1][:, :], in_=bank[0][:, :], func=AF.Ln, bias=epscol[:, 0:1],
                   scale=1.0, reads=[("bank", 0), ("epscol",)], writes=[("tf", 1)])
                op("act", "activation", out=tf[1][:, :], in_=tf[1][:, :], func=AF.Exp, scale=-0.5,
                   reads=[("tf", 1)], writes=[("tf", 1)])
                op("dve", "scalar_tensor_tensor", out=mixT[:, mch, gs], in0=tf[0][:, :], scalar=gvec,
                   in1=tf[1][:, :], op0=ALU.mult, op1=ALU.mult, reads=[("tf", 0), ("tf", 1), gkey],
                   writes=[("RA", mch, g)])

            deferred.append(part_b)
        run_deferred()

        for dc in range(8):
            s = load_wA(w_out_d[l * 8 + dc])
            for g in range(NG):
                gs = slice(g * 512, (g + 1) * 512)
                b = nxt("mmb", 6)
                for ec in range(8):
                    mm(bank[b][:, :], wA[s][:, ec, :], mixT[:, ec, gs], ec == 0, ec == 7,
                       reads=[("RA", ec, g), ("wA", s)], writes=[("bank", b)])
                op("dve", "tensor_tensor", out=xT[:, dc, gs], in0=bank[b][:, :], in1=xT[:, dc, gs], op=ALU.add,
                   reads=[("bank", b), ("xT", dc, g)], writes=[("xT", dc, g)])

        op("dve", "memset", onecol[:, :], 1.0, reads=[("RA",), ("RB",), ("qT",)], writes=[("RA",), ("RB",), ("qT",), ("onecol",)])
        for g in range(NG):
            gs = slice(g * 512, (g + 1) * 512)
            nb = 6 + nxt("nb", 2)
            t = rmsnorm_stats(lambda kc: xT[:, kc, gs], lambda kc: ("xT", kc, g), g, nb)
            for kc in range(8):
                op("dve", "scalar_tensor_tensor", out=h2T[:, kc, gs], in0=xT[:, kc, gs],
                   scalar=V("g2", l * 8 + kc), in1=tf[t][:, :], op0=ALU.mult, op1=ALU.mult,
                   reads=[("xT", kc, g), ("tf", t), ("vecs",)], writes=[("RA", "h2", kc, g)])
        for fh in range(2):
            for fl in range(NFH):
                fc = fh * NFH + fl
                sg = load_wA(w_gate_d[l * NFC + fc])
                su = load_wA(w_up_d[l * NFC + fc])
                for g in range(NG):
                    gs = slice(g * 512, (g + 1) * 512)
                    bg = nxt("mmb", 6)
                    for kc in range(8):
                        mm(bank[bg][:, :], wA[sg][:, kc, :], h2T[:, kc, gs], kc == 0, kc == 7,
                           reads=[("RA", "h2", kc, g), ("wA", sg)], writes=[("bank", bg)])
                    bu = nxt("mmb", 6)
                    for kc in range(8):
                        mm(bank[bu][:, :], wA[su][:, kc, :], h2T[:, kc, gs], kc == 0, kc == 7,
                           reads=[("RA", "h2", kc, g), ("wA", su)], writes=[("bank", bu)])
                    t = nxt("tf", 4)
                    op("act", "activation", out=tf[t][:, :], in_=bank[bg][:, :], func=AF.Silu,
                       reads=[("bank", bg)], writes=[("tf", t)])
                    op("dve", "tensor_tensor", out=actT(fl)[:, gs], in0=bank[bu][:, :], in1=tf[t][:, :], op=ALU.mult,
                       reads=[("bank", bu), ("tf", t)], writes=[actkey(fl, g)])
            for dc in range(8):
                sd = nxt("dn", 2)
                op("pool", "dma_start", out=wDn[sd][:, :, :],
                   in_=w_down_d[l * 8 + dc][:, fh * NFH:(fh + 1) * NFH, :],
                   writes=[("qT", "dn", sd)], chan=f"wD{sd}")
                for g in range(NG):
                    gs = slice(g * 512, (g + 1) * 512)
                    b = nxt("mmb", 6)
                    for fl in range(NFH):
                        mm(bank[b][:, :], wDn[sd][:, fl, :], actT(fl)[:, gs], fl == 0, fl == NFH - 1,
                           reads=[actkey(fl, g), ("qT", "dn", sd)], writes=[("bank", b)])
                    op("dve", "tensor_tensor", out=xT[:, dc, gs], in0=bank[b][:, :], in1=xT[:, dc, gs], op=ALU.add,
                       reads=[("bank", b), ("xT", dc, g)], writes=[("xT", dc, g)])
        op("dve", "memset", onecol[:, :], 1.0, reads=[("RA",), ("RB",), ("qT",)], writes=[("RA",), ("RB",), ("qT",), ("onecol",)])

    for g in range(NG):
        gs = slice(g * 512, (g + 1) * 512)
        nb = 6 + nxt("nb", 2)
        t = rmsnorm_stats(lambda kc: xT[:, kc, gs], lambda kc: ("xT", kc, g), g, nb, tslot=g % 2)
        for kc in range(8):
            o = nxt("ot", 2)
            op("dve", "scalar_tensor_tensor", out=ot[o][:, :], in0=xT[:, kc, gs],
               scalar=V("gf", kc), in1=tf[t][:, :], op0=ALU.mult, op1=ALU.mult,
               reads=[("xT", kc, g), ("tf", t), ("vecs",)], writes=[("tf", 2 + o)])
            op("sp", "dma_start", out=outT_d[kc * 128:(kc + 1) * 128, gs], in_=ot[o][:, :],
               reads=[("tf", 2 + o)], writes=[("out", kc, g)], chan=f"out{o}")

    print("sbuf bytes remaining per partition:", nc.sbuf_bytes_remaining)
    chans = list(T.chan_ops.keys())
    sem_guards = []

    def mksem(name):
        g_ = nc.semaphore(name)
        sem_guards.append(g_)
        return g_.__enter__()

    esem = {e: mksem(f"e_{e}") for e in ("pe", "act", "dve", "pool")}
    csem = {c: mksem(f"c_{c}") for c in chans}
    with nc.Block() as block:
        @block.sync
        def _(e):
            T.emit("sp", e, esem, csem)
            T.final_waits(e, esem, csem)

        @block.tensor
        def _(e):
            T.emit("pe", e, esem, csem)

        @block.scalar
        def _(e):
            T.emit("act", e, esem, csem)

        @block.vector
        def _(e):
            T.emit("dve", e, esem, csem)

        @block.gpsimd
        def _(e):
            T.emit("pool", e, esem, csem)
    for g_ in reversed(sem_guards):
        g_.__exit__(None, None, None)
    for g_ in reversed(guards):
        g_.__exit__(None, None, None)
    return nc, T


def _const_mats(j):
    bf = ml_dtypes.bfloat16
    k = np.arange(128)[:, None]
    q = np.arange(128)[None, :]
    m = np.zeros((NCM, 128, 128), np.float32)
    m[C_ONESD] = 1.0 / D
    m[C_ONES128] = 1.0 / 128
    m[C_ONES64B][:64, :64] = 1.0 / 64
    m[C_ONES64B][64:, 64:] = 1.0 / 64
    m[C_ONE] = 1.0
    m[C_IDENT] = np.eye(128)
    m[C_NEGTRI] = np.where(k >= q, -1.0, 0.0)
    m[C_NEGONES] = -1.0
    diag_d = np.where(k <= q, 0.0, NEG)
    diag_s = np.where(k < q, 0.0, NEG)
    full = np.zeros((128, 128))
    none = np.full((128, 128), NEG)
    if j == 0:
        m[C_MA_D], m[C_MB_D], m[C_MA_S], m[C_MB_S] = diag_d, none, diag_s, none
    else:
        m[C_MA_D], m[C_MB_D], m[C_MA_S], m[C_MB_S] = full, diag_d, full, diag_s
    return np.ascontiguousarray(m.transpose(1, 0, 2).reshape(128, NCM * 128)).astype(bf)


def _pack_vecs(depth, j, norm1_g, norm2_g, final_norm_g, pool_scale, diff_norm_g, sb_norm_g,
               lam_q1, lam_k1, lam_q2, lam_k2):
    voff, NV = vec_layout(depth)
    v = np.zeros((128, NV), np.float32)
    p = np.arange(128)
    for l in range(depth):
        for kc in range(8):
            v[:, voff["g1"] + l * 8 + kc] = norm1_g[l, kc * 128:(kc + 1) * 128]
            v[:, voff["g2"] + l * 8 + kc] = norm2_g[l, kc * 128:(kc + 1) * 128]
        for c in range(2):
            v[:, voff["pscale"] + l * 2 + c] = pool_scale[l, c * 128:(c + 1) * 128]
        v[:, voff["gdiff"] + l] = diff_norm_g[l]
        v[:, voff["gsb"] + l] = sb_norm_g[l][p % 64]
        for i, a in enumerate((lam_q1, lam_k1, lam_q2, lam_k2)):
            o = voff["lam"] + (l * 4 + i) * 64
            v[:, o:o + 64] = a[l][None, :]
    for kc in range(8):
        v[:, voff["gf"] + kc] = final_norm_g[kc * 128:(kc + 1) * 128]
    for c in range(2):
        w = np.array([POOL_WINDOWS[2 * c + pp // 64] for pp in p], np.float32)
        v[:, voff["invw"] + c] = 1.0 / w
        for t in range(16):
            cnt = np.minimum(t + 1, w) if j == 0 else w
            v[:, voff["coef"] + c * 16 + t] = 1.0 / cnt
    v[:, voff["sel"] + 0] = 0.0 if j == 0 else 1.0
    v[:, voff["sel"] + 1] = 1.0 if j == 0 else 0.0
    return v


def _pack_poolw(depth, pool_w):
    m = np.zeros((128, depth * 2, 128), np.float32)
    for l in range(depth):
        for c in range(2):
            m[0:64, l * 2 + c, 0:64] = pool_w[l, 2 * c]
            m[64:128, l * 2 + c, 64:128] = pool_w[l, 2 * c + 1]
    return np.ascontiguousarray(m.reshape(128, depth * 2 * 128))


_CACHE = {}


def run(x, norm1_g, w_in, pool_w, pool_scale, lam_q1, lam_k1, lam_q2, lam_k2, diff_norm_g, sb_norm_g,
        w_out, norm2_g, w_gate, w_up, w_down, final_norm_g, trace=False):
    f = lambda a: np.ascontiguousarray(np.asarray(a, dtype=np.float32))
    x = f(x)
    B, S, _ = x.shape
    depth = int(np.asarray(w_in).shape[0])
    assert B == 4 and S % 1024 == 0
    NB = S // 256
    TL = NB * 128
    key = (S, depth)
    if key not in _CACHE:
        _CACHE[key] = build(S, depth)[0]
    nc = _CACHE[key]
    def tile_w(w, kchunks, nchunks):
        w = f(w).reshape(depth, kchunks, 128, nchunks, 128).transpose(0, 3, 2, 1, 4)
        return np.ascontiguousarray(w).reshape(depth * nchunks, 128, kchunks, 128)

    w_in, w_out = tile_w(w_in, 8, 20), tile_w(w_out, 8, 8)
    w_gate, w_up, w_down = tile_w(w_gate, 8, NFC), tile_w(w_up, 8, NFC), tile_w(w_down, NFC, 8)
    pw = _pack_poolw(depth, f(pool_w))
    in_maps = []
    for c in range(8):
        b, j = c // 2, c % 2
        xs = x[b].reshape(NB, 2, 128, D)[:, j].reshape(TL, D)
        in_maps.append({
            "xT": np.ascontiguousarray(xs.T),
            "w_in": w_in, "w_out": w_out, "w_gate": w_gate, "w_up": w_up, "w_down": w_down,
            "vecs": _pack_vecs(depth, j, f(norm1_g), f(norm2_g), f(final_norm_g), f(pool_scale),
                               f(diff_norm_g), f(sb_norm_g), f(lam_q1), f(lam_k1), f(lam_q2), f(lam_k2)),
            "poolw": pw,
            "cmats": _const_mats(j),
        })
    res = run_bass_kernel_spmd(nc, in_maps, core_ids=list(range(8)), **({"trace": True} if trace else {}))
    out = np.empty((B, S, D), np.float32)
    for c in range(8):
        b, j = c // 2, c % 2
        o = np.asarray(res.results[c]["outT"], dtype=np.float32).T.reshape(NB, 128, D)
        out[b].reshape(NB, 2, 128, D)[:, j] = o
    return out, res


def kernel(**inputs):
    out, _ = run(**inputs)
    return out
```

```python
import bisect
import math

import ml_dtypes
import numpy as np

import concourse.bass as bass
import concourse.mybir as mybir
from concourse.bass_utils import run_bass_kernel_spmd

F32 = mybir.dt.float32
BF16 = mybir.dt.bfloat16
AF = mybir.ActivationFunctionType
ALU = mybir.AluOpType
AX = mybir.AxisListType

D = 1024
D_IN = 2560
D_FF = 2816
NFC = D_FF // 128
EPS = 1e-6
POOL_WINDOWS = (2, 4, 8, 16)
NEG = -30000.0
NCM = 11
(C_ONESD, C_ONES128, C_ONES64B, C_ONE, C_IDENT, C_NEGTRI, C_NEGONES,
 C_MA_D, C_MB_D, C_MA_S, C_MB_S) = range(NCM)


class Tracker:
    ENGS = ("pe", "act", "dve", "pool", "sp")

    def __init__(self):
        self.ops = []
        self.by_eng = {e: [] for e in self.ENGS}
        self.ncomp = {e: 0 for e in self.ENGS}
        self.state = {}
        self.children = {}
        self.chan_ops = {}

    def _related(self, key):
        out = []
        for n in range(1, len(key)):
            p = key[:n]
            if p in self.state:
                out.append(p)
        if key in self.state:
            out.append(key)
        out.extend(self.children.get(key, ()))
        return out

    def _register(self, key):
        if key not in self.state:
            self.state[key] = [None, []]
            for n in range(1, len(key)):
                self.children.setdefault(key[:n], set()).add(key)

    def add(self, eng, fn, reads=(), writes=(), chan=None):
        oid = len(self.ops)
        deps = {}
        for r in reads:
            for k in self._related(r):
                w = self.state[k][0]
                if w is not None:
                    deps[w] = "raw"
        for w_ in writes:
            for k in self._related(w_):
                st = self.state[k]
                if st[0] is not None:
                    deps.setdefault(st[0], "waw")
                for rd in st[1]:
                    deps.setdefault(rd, "war")
        for r in reads:
            self._register(r)
            self.state[r][1].append(oid)
        for w_ in writes:
            self._register(w_)
            self.state[w_] = [oid, []]
            for k in self.children.get(w_, ()):
                self.state[k] = [oid, []]
        deps.pop(oid, None)
        op = dict(id=oid, eng=eng, fn=fn, deps=deps, chan=chan)
        if chan is not None:
            self.chan_ops.setdefault(chan, []).append(oid)
        else:
            op["cidx"] = self.ncomp[eng]
            self.ncomp[eng] += 1
        self.ops.append(op)
        self.by_eng[eng].append(oid)
        return oid

    def emit(self, engname, eng, esem, csem):
        seen = {}
        for oid in self.by_eng[engname]:
            op = self.ops[oid]
            waits = {}
            for d, kind in op["deps"].items():
                dop = self.ops[d]
                if dop["chan"] is not None:
                    key = ("c", dop["chan"])
                    val = 16 * bisect.bisect_left(self.chan_ops[dop["chan"]], oid)
                else:
                    if dop["eng"] == engname and (engname == "pe" or kind == "war"):
                        continue
                    key = ("e", dop["eng"])
                    val = dop["cidx"] + 1
                if waits.get(key, 0) < val:
                    waits[key] = val
            for key, val in waits.items():
                if seen.get(key, 0) >= val:
                    continue
                seen[key] = val
                eng.wait_ge(csem[key[1]] if key[0] == "c" else esem[key[1]], val)
            ins = op["fn"](eng)
            if op["chan"] is not None:
                ins.then_inc(csem[op["chan"]], 16)
            else:
                ins.then_inc(esem[engname], 1)

    def final_waits(self, eng, esem, csem):
        for ch, lst in self.chan_ops.items():
            eng.wait_ge(csem[ch], 16 * len(lst))
        for e in ("pe", "act", "dve", "pool"):
            if self.ncomp[e]:
                eng.wait_ge(esem[e], self.ncomp[e])


def vec_layout(depth):
    off = {}
    n = 0
    for name, w in (("g1", depth * 8), ("g2", depth * 8), ("gf", 8), ("pscale", depth * 2),
                    ("gdiff", depth), ("gsb", depth), ("invw", 2), ("sel", 2), ("coef", 32),
                    ("lam", depth * 4 * 64)):
        off[name] = n
        n += w
    return off, n


def build(S, depth):
    NB = S // 256
    TL = NB * 128
    NG = TL // 512
    XW = 14 * TL
    KOFF, VOFF, UOFF = 0, 6 * TL, 12 * TL
    voff, NV = vec_layout(depth)
    HT = min(TL, 1024)
    NHF = TL // HT
    GH = HT // 512

    nc = bass.Bass("TRN2", target_bir_lowering=False)
    xT_d = nc.dram_tensor("xT", [D, TL], F32, kind="ExternalInput").ap()
    w_in_d = nc.dram_tensor("w_in", [depth * 20, 128, 8, 128], F32, kind="ExternalInput").ap()
    w_out_d = nc.dram_tensor("w_out", [depth * 8, 128, 8, 128], F32, kind="ExternalInput").ap()
    w_gate_d = nc.dram_tensor("w_gate", [depth * NFC, 128, 8, 128], F32, kind="ExternalInput").ap()
    w_up_d = nc.dram_tensor("w_up", [depth * NFC, 128, 8, 128], F32, kind="ExternalInput").ap()
    w_down_d = nc.dram_tensor("w_down", [depth * 8, 128, NFC, 128], F32, kind="ExternalInput").ap()
    vec_d = nc.dram_tensor("vecs", [128, NV], F32, kind="ExternalInput").ap()
    pw_d = nc.dram_tensor("poolw", [128, depth * 2 * 128], F32, kind="ExternalInput").ap()
    cm_d = nc.dram_tensor("cmats", [128, NCM * 128], BF16, kind="ExternalInput").ap()
    outT_d = nc.dram_tensor("outT", [D, TL], F32, kind="ExternalOutput").ap()
    snd_ap = [nc.dram_tensor(f"snd{i}", [128, 2 * TL], BF16).ap() for i in range(7)]
    rcv_ap = [nc.dram_tensor(f"rcv{i}", [256, 2 * TL], BF16).ap() for i in range(7)]

    T = Tracker()
    guards = []

    def sb(name, shape, dt):
        g = nc.sbuf_tensor(name, shape, dt)
        guards.append(g)
        return g.__enter__()

    def psb(name):
        g = nc.psum_tensor(name, [128, 512], F32)
        guards.append(g)
        return g.__enter__()

    NFH = NFC // 2
    RBW = 8 * TL
    xT = sb("xT_sb", [128, 8, TL], F32)
    RA = sb("RA", [128, 8 * TL], BF16)
    RB = sb("RB", [128, RBW], BF16)
    QW = max(6 * TL, 2 * NFH * 128 + (NFH - 8) * TL)
    qTr = sb("qT", [128, QW], BF16)
    qT = qTr[:, 0:6 * TL].rearrange("p (c t) -> p c t", c=6)
    uT = sb("uT", [128, 2, TL], BF16)
    wA = [sb(f"wA{i}", [128, 8, 128], BF16) for i in range(4)]
    vecs = sb("vecs_sb", [128, NV], F32)
    pwf = sb("pwf", [128, depth * 2 * 128], F32)
    pwb = sb("pwb", [128, depth * 2, 128], BF16)
    cm = sb("cm", [128, NCM, 128], BF16)
    neglam = sb("neglam", [128, depth], F32)
    gdiffs = sb("gdiffs", [128, depth], F32)
    lamtmp = sb("lamtmp", [128, 2, 64], F32)
    lams = sb("lams", [128, 4], F32)
    onecol = sb("onecol", [128, 1], F32)
    epscol = sb("epscol", [128, 1], F32)
    sqb = [sb(f"sqb{i}", [128, 512], BF16) for i in range(2)]
    tf_all = sb("tf_all", [128, 4, 512], F32)
    tf = [tf_all[:, i, :] for i in range(4)]
    Pt_all = sb("Pt_all", [128, 4, 512], BF16)
    Pt = [Pt_all[:, i, :] for i in range(4)]
    Et = [tf[2], tf[3]]
    ot = [tf[2], tf[3]]
    qz = [[sb(f"qz{r}_{i}", [128, 512], BF16) for i in range(2)] for r in range(2)]
    Lt_all = sb("Lt_all", [128, 4, 512], BF16)
    Lt = [Lt_all[:, i, :] for i in range(4)]
    Ls_all = sb("Ls_all", [128, 2, 512], BF16)
    Ls = [Ls_all[:, i, :] for i in range(2)]
    uext = sb("uext", [128, 4, 144], F32)
    s2 = sb("s2", [128, 4, 144], F32)
    s4 = sb("s4", [128, 4, 144], F32)
    g_ps = nc.psum_tensor("ps_all", [128, 8, 512], F32)
    guards.append(g_ps)
    ps_all = g_ps.__enter__()
    bank = [ps_all[:, i, :] for i in range(8)]

    hT = RA[:, :].rearrange("p (c t) -> p c t", c=8)
    mixT = hT
    v_loc = RB[:, 0:6 * TL].rearrange("p (h b e) -> p h b e", h=6, b=NB)
    kst = [RB[:, 6 * TL + i * TL: 6 * TL + (i + 1) * TL] for i in range(2)]
    KTs = [RB[:, s * 4 * TL: s * 4 * TL + 2 * TL].rearrange("p (r t) -> p r t", r=2) for s in range(2)]
    Vs = [RB[:, s * 4 * TL + 2 * TL: (s + 1) * 4 * TL].rearrange("p (r b e) -> p r b e", r=2, b=NB)
          for s in range(2)]
    uall = RB[:, 4 * TL: 8 * TL].rearrange("p (r c t) -> p r c t", r=2, c=2)
    h2T = hT

    def actT(fl):
        if fl < 8:
            return RB[:, fl * TL:(fl + 1) * TL]
        o = 2 * NFH * 128 + (fl - 8) * TL
        return qTr[:, o:o + TL]

    def actkey(fl, g):
        return ("RB", "act", fl, g) if fl < 8 else ("qT", "act", fl, g)

    wDn = [qTr[:, i * NFH * 128:(i + 1) * NFH * 128].rearrange("p (f n) -> p f n", f=NFH) for i in range(2)]

    def V(name, i=0):
        return vecs[:, voff[name] + i: voff[name] + i + 1]

    def CM(i):
        return cm[:, i, :]

    def op(eng, method, *args, reads=(), writes=(), chan=None, **kw):
        return T.add(eng, lambda e: getattr(e, method)(*args, **kw), reads=reads, writes=writes, chan=chan)

    def mm(out, lhsT, rhs, start, stop, reads, writes):
        return T.add("pe", lambda e: e.matmul(out, lhsT, rhs, start=start, stop=stop), reads=reads, writes=writes)

    rot = {"qz": 0, "P": 0, "wA": 0, "mmb": 0, "ev": 0, "sq": 0, "nb": 0, "tf": 0, "dn": 0, "ot": 0}

    def nxt(k, n):
        v = rot[k] % n
        rot[k] += 1
        return v

    op("sp", "dma_start", out=vecs[:, :], in_=vec_d[:, :], writes=[("vecs",)], chan="const")
    op("sp", "dma_start", out=pwf[:, :], in_=pw_d[:, :], writes=[("pwf",)], chan="const")
    op("sp", "dma_start", out=cm[:, :, :], in_=cm_d.rearrange("p (m n) -> p m n", m=NCM),
       writes=[("cm",)], chan="const")
    for kc in range(8):
        op("sp", "dma_start", out=xT[:, kc, :], in_=xT_d[kc * 128:(kc + 1) * 128, :],
           writes=[("xT", kc)], chan="xin")
    op("dve", "memset", onecol[:, :], 1.0, writes=[("onecol",)])
    op("dve", "memset", epscol[:, :], EPS, writes=[("epscol",)])
    for r in range(2):
        for i in range(2):
            op("dve", "memset", qz[r][i][:, :], 0.0, writes=[("qz", i)])
    op("dve", "tensor_copy", out=pwb[:, :, :], in_=pwf[:, :].rearrange("p (m n) -> p m n", n=128),
       reads=[("pwf",)], writes=[("pwb",)])
    for l in range(depth):
        lam_init = 0.8 - 0.6 * math.exp(-0.3 * l)
        lo = voff["lam"] + l * 256
        lv = vecs[:, lo:lo + 256].rearrange("p (i d) -> p i d", i=4)
        op("dve", "tensor_tensor", out=lamtmp[:, 0, :], in0=lv[:, 0, :], in1=lv[:, 1, :], op=ALU.mult,
           reads=[("vecs",)], writes=[("lamtmp",)])
        op("dve", "tensor_tensor", out=lamtmp[:, 1, :], in0=lv[:, 2, :], in1=lv[:, 3, :], op=ALU.mult,
           reads=[("vecs",)], writes=[("lamtmp",)])
        op("dve", "reduce_sum", out=lams[:, 0:2], in_=lamtmp[:, :, :], axis=AX.X,
           reads=[("lamtmp",)], writes=[("lams",)])
        op("act", "activation", out=lams[:, 2:4], in_=lams[:, 0:2], func=AF.Exp,
           reads=[("lams",)], writes=[("lams",)])
        op("dve", "tensor_tensor", out=neglam[:, l:l + 1], in0=lams[:, 3:4], in1=lams[:, 2:3], op=ALU.subtract,
           reads=[("lams",)], writes=[("neglam", l)])
        op("dve", "tensor_scalar", out=neglam[:, l:l + 1], in0=neglam[:, l:l + 1], scalar1=-lam_init,
           scalar2=None, op0=ALU.add, reads=[("neglam", l)], writes=[("neglam", l)])
        op("dve", "tensor_scalar", out=gdiffs[:, l:l + 1], in0=V("gdiff", l), scalar1=1.0 - lam_init,
           scalar2=None, op0=ALU.mult, reads=[("vecs",)], writes=[("gdiffs", l)])

    def rmsnorm_stats(src_fn, src_keys, g, nbank, ones_idx=C_ONESD, tslot=None):
        b = bank[nbank]
        for kc in range(8):
            s = nxt("sq", 2)
            op("act", "activation", out=sqb[s][:, :], in_=src_fn(kc), func=AF.Square,
               reads=[src_keys(kc)], writes=[("sqb", s)])
            mm(b[:, :], CM(ones_idx), sqb[s][:, :], kc == 0, kc == 7,
               reads=[("sqb", s), ("cm",)], writes=[("bank", nbank)])
        t = nxt("tf", 4) if tslot is None else tslot
        op("act", "activation", out=tf[t][:, :], in_=b[:, :], func=AF.Ln, bias=epscol[:, 0:1], scale=1.0,
           reads=[("bank", nbank), ("epscol",)], writes=[("tf", t)])
        op("act", "activation", out=tf[t][:, :], in_=tf[t][:, :], func=AF.Exp, scale=-0.5,
           reads=[("tf", t)], writes=[("tf", t)])
        return t

    def load_wA(src_ap):
        s = nxt("wA", 4)
        op("pool", "dma_start", out=wA[s][:, :, :], in_=src_ap, writes=[("wA", s)], chan=f"wA{s}")
        flush_ag(3)
        return s

    pending_ag = []

    def flush_ag(age):
        while pending_ag and rot["wA"] - pending_ag[0][1] >= age:
            i = pending_ag.pop(0)[0]
            T.add("pool", lambda e, i=i: e.collective_compute(
                "AllGather", ALU.bypass, replica_groups=[[0, 1], [2, 3], [4, 5], [6, 7]],
                ins=[snd_ap[i].opt()], outs=[rcv_ap[i].opt()]), reads=[("snd", i)], writes=[("rcv", i)])

    def evac(out_ap, in_ap, reads, writes, scale=None):
        e = nxt("ev", 2)
        if e == 0:
            if scale is None:
                op("act", "copy", out=out_ap, in_=in_ap, reads=reads, writes=writes)
            else:
                op("act", "mul", out=out_ap, in_=in_ap, mul=scale, reads=reads, writes=writes)
        else:
            if scale is None:
                op("dve", "tensor_copy", out=out_ap, in_=in_ap, reads=reads, writes=writes)
            else:
                op("dve", "tensor_scalar", out=out_ap, in0=in_ap, scalar1=scale, scalar2=None,
                   op0=ALU.mult, reads=reads, writes=writes)

    for l in range(depth):
        for g in range(NG):
            gs = slice(g * 512, (g + 1) * 512)
            nb = 6 + nxt("nb", 2)
            t = rmsnorm_stats(lambda kc: xT[:, kc, gs], lambda kc: ("xT", kc, g), g, nb)
            for kc in range(8):
                op("dve", "scalar_tensor_tensor", out=hT[:, kc, gs], in0=xT[:, kc, gs],
                   scalar=V("g1", l * 8 + kc), in1=tf[t][:, :], op0=ALU.mult, op1=ALU.mult,
                   reads=[("xT", kc, g), ("tf", t), ("vecs",)], writes=[("RA", kc, g)])
        def allgather(i):
            pending_ag.append((i, rot["wA"]))

        def fm_chunk(cc, dest):
            s = load_wA(w_in_d[l * 20 + cc])
            for g in range(NG):
                gs = slice(g * 512, (g + 1) * 512)
                b = nxt("mmb", 6)
                for kc in range(8):
                    mm(bank[b][:, :], wA[s][:, kc, :], hT[:, kc, gs], kc == 0, kc == 7,
                       reads=[("RA", kc, g), ("wA", s)], writes=[("bank", b)])
                dest(g, gs, b)

        for hcv in range(6):
            ccv = 10 + hcv if hcv < 4 else 18 + (hcv - 4)
            s = load_wA(w_in_d[l * 20 + ccv])
            for g in range(NG):
                b = nxt("mmb", 6)
                for ib in range(4):
                    i = g * 4 + ib
                    for kc in range(8):
                        mm(bank[b][:, ib * 128:(ib + 1) * 128], hT[:, kc, i * 128:(i + 1) * 128], wA[s][:, kc, :],
                           kc == 0, kc == 7, reads=[("RA", kc, g), ("wA", s)], writes=[("bank", b)])
                evac(v_loc[:, hcv, g * 4:(g + 1) * 4, :], bank[b][:, :].rearrange("p (b e) -> p b e", e=128),
                     reads=[("bank", b)], writes=[("RB", "v", hcv, g)])
            op("sp", "dma_start", out=snd_ap[hcv][:, TL:2 * TL], in_=RB[:, hcv * TL:(hcv + 1) * TL],
               reads=[("RB", "v", hcv)], writes=[("snd", hcv, "v")], chan=f"snd{hcv}")
            cc = 6 + hcv if hcv < 4 else 16 + (hcv - 4)
            ks = hcv % 2
            fm_chunk(cc, lambda g, gs, b: evac(kst[ks][:, gs], bank[b][:, :], reads=[("bank", b)],
                                               writes=[("RB", "k", ks, g)]))
            op("sp", "dma_start", out=snd_ap[hcv][:, 0:TL], in_=kst[ks],
               reads=[("RB", "k", ks)], writes=[("snd", hcv, "k")], chan=f"snd{hcv}")
            allgather(hcv)
        for cc in range(2):
            fm_chunk(cc, lambda g, gs, b: evac(uT[:, cc, gs], bank[b][:, :], reads=[("bank", b)],
                                               writes=[("uT", cc, g)]))
        op("sp", "dma_start", out=snd_ap[6][:, :], in_=uT[:, :, :].rearrange("p c t -> p (c t)"),
           reads=[("uT",)], writes=[("snd", 6)], chan="snd6")
        allgather(6)
        for qc in range(6):
            cc = 2 + qc if qc < 4 else 14 + (qc - 4)
            fm_chunk(cc, lambda g, gs, b: evac(qT[:, qc, gs], bank[b][:, :], reads=[("bank", b)],
                                               writes=[("qT", qc, g)], scale=0.125))

        flush_ag(0)
        op("dve", "memset", onecol[:, :], 1.0, reads=[("RA",), ("RB",), ("qT",)], writes=[("RA",), ("RB",), ("qT",), ("onecol",)])

        def load_kv(hc):
            s = hc % 2
            for r in range(2):
                op("sp", "dma_start", out=KTs[s][:, r, :], in_=rcv_ap[hc][r * 128:(r + 1) * 128, 0:TL],
                   reads=[("rcv", hc)], writes=[("RB", "kv", s)], chan=f"kv{s}")
                op("sp", "dma_start", out=Vs[s][:, r, :, :],
                   in_=rcv_ap[hc][r * 128:(r + 1) * 128, TL:2 * TL].rearrange("p (b e) -> p b e", e=128),
                   reads=[("rcv", hc)], writes=[("RB", "kv", s)], chan=f"kv{s}")

        load_kv(0)
        for r in range(2):
            op("sp", "dma_start", out=uall[:, r, :, :],
               in_=rcv_ap[6][r * 128:(r + 1) * 128, :].rearrange("p (c t) -> p c t", c=2),
               reads=[("rcv", 6)], writes=[("RB", "kv", 1)], chan="kv1")
        uT4 = uT[:, :, :].rearrange("p c (b t) -> p c b t", t=128)
        ua4 = [uall[:, r, :, :].rearrange("p c (b t) -> p c b t", t=128) for r in range(2)]
        for c in range(2):
            for g in range(NG):
                gs = slice(g * 512, (g + 1) * 512)
                b0 = g * 4
                op("dve", "tensor_copy", out=uext[:, :, 16:144], in_=uT4[:, c, b0:b0 + 4, :],
                   reads=[("uT", c, g)], writes=[("uext",)])
                op("dve", "tensor_scalar", out=uext[:, :, 0:16], in0=ua4[0][:, c, b0:b0 + 4, 112:128],
                   scalar1=V("sel", 0), scalar2=None, op0=ALU.mult,
                   reads=[("RB", "kv", 1), ("vecs",)], writes=[("uext",)])
                lo_b = 1 if b0 == 0 else 0
                op("dve", "scalar_tensor_tensor", out=uext[:, lo_b:4, 0:16],
                   in0=ua4[1][:, c, b0 + lo_b - 1:b0 + 3, 112:128], scalar=V("sel", 1),
                   in1=uext[:, lo_b:4, 0:16], op0=ALU.mult, op1=ALU.add,
                   reads=[("RB", "kv", 1), ("vecs",), ("uext",)], writes=[("uext",)])
                op("dve", "tensor_tensor", out=s2[:, :, 1:144], in0=uext[:, :, 1:144], in1=uext[:, :, 0:143],
                   op=ALU.add, reads=[("uext",)], writes=[("s2",)])
                op("dve", "tensor_tensor", out=s4[:, :, 3:144], in0=s2[:, :, 3:144], in1=s2[:, :, 1:142],
                   op=ALU.add, reads=[("s2",)], writes=[("s4",)])
                if c == 1:
                    op("dve", "tensor_tensor", out=s2[:, :, 7:144], in0=s4[:, :, 7:144], in1=s4[:, :, 3:140],
                       op=ALU.add, reads=[("s4",), ("s2",)], writes=[("s2",)])
                    op("dve", "tensor_tensor", out=s4[:, :, 15:144], in0=s2[:, :, 15:144], in1=s2[:, :, 7:136],
                       op=ALU.add, reads=[("s2",), ("s4",)], writes=[("s4",)])
                p = nxt("P", 4)
                pl = Pt[p][:, :].rearrange("p (b t) -> p b t", t=128)
                for (lo, hi, src_t) in ((0, 64, s2), (64, 128, s4)):
                    op("dve", "scalar_tensor_tensor", out=pl[lo:hi, :, :], in0=src_t[lo:hi, :, 16:144],
                       scalar=vecs[lo:hi, voff["invw"] + c:voff["invw"] + c + 1], in1=uext[lo:hi, :, 16:144],
                       op0=ALU.mult, op1=ALU.subtract, reads=[("s2",), ("s4",), ("uext",), ("vecs",)],
                       writes=[("Pt", p)])
                    if g == 0:
                        op("dve", "tensor_tensor", out=tf[0][lo:hi, 0:16], in0=src_t[lo:hi, 0, 16:32],
                           in1=vecs[lo:hi, voff["coef"] + c * 16:voff["coef"] + c * 16 + 16], op=ALU.mult,
                           reads=[("s2",), ("s4",), ("vecs",)], writes=[("tf", 0)])
                        op("dve", "tensor_tensor", out=pl[lo:hi, 0, 0:16], in0=tf[0][lo:hi, 0:16],
                           in1=uext[lo:hi, 0, 16:32], op=ALU.subtract, reads=[("tf", 0), ("uext",)],
                           writes=[("Pt", p)])
                b = nxt("mmb", 4)
                mm(bank[b][:, :], pwb[:, l * 2 + c, :], Pt[p][:, :], True, True,
                   reads=[("Pt", p), ("pwb",)], writes=[("bank", b)])
                op("dve", "tensor_scalar", out=mixT[:, c, gs], in0=bank[b][:, :], scalar1=V("pscale", l * 2 + c),
                   scalar2=None, op0=ALU.mult, reads=[("bank", b), ("vecs",)], writes=[("RA", c, g)])

        iters = [(hc, g) for hc in range(6) for g in range(NG)]
        deferred = []

        def prep_qz(hc, g):
            qs = nxt("qz", 2)
            gs = slice(g * 512, (g + 1) * 512)
            op("dve", "tensor_copy", out=qz[0][qs][0:64, :], in_=qT[0:64, hc, gs],
               reads=[("qT", hc, g)], writes=[("qz", qs)])
            op("dve", "tensor_copy", out=qz[1][qs][64:128, :], in_=qT[64:128, hc, gs],
               reads=[("qT", hc, g)], writes=[("qz", qs)])
            return qs

        def run_deferred():
            while deferred:
                deferred.pop(0)()

        qs_next = prep_qz(*iters[0])
        for it, (hc, g) in enumerate(iters):
            s = hc % 2
            if g == 0 and hc + 1 < 6:
                load_kv(hc + 1)
            is_diff = hc < 4
            mA = CM(C_MA_D if is_diff else C_MA_S)
            mB = CM(C_MB_D if is_diff else C_MB_S)
            mch = 2 + hc
            nkb = 8 * g + 8
            gs = slice(g * 512, (g + 1) * 512)
            qs = qs_next
            if it + 1 < len(iters):
                qs_next = prep_qz(*iters[it + 1])

            def c0_of(kb):
                return (max(4 * g, kb // 2) - 4 * g) * 128

            def qk_pair(sl, kb, stop=True):
                c0 = c0_of(kb)
                rk, lb = kb % 2, kb // 2
                im = kb // 2
                has_mask = im >= 4 * g
                for r in range(2):
                    bk = 2 * sl + r
                    mm(bank[bk][:, c0:512], KTs[s][:, rk, lb * 128:(lb + 1) * 128], qz[r][qs][:, c0:512],
                       True, stop and not has_mask, reads=[("RB", "kv", s), ("qz", qs)], writes=[("bank", bk)])
                    if has_mask:
                        mc = (im - 4 * g) * 128
                        mm(bank[bk][:, mc:mc + 128], CM(C_IDENT), (mA if kb % 2 == 0 else mB), False, stop,
                           reads=[("cm",)], writes=[("bank", bk)])

            def vblk(kb):
                return Vs[s][:, kb % 2, kb // 2, :]

            def bk2(sl):
                return [("bank", 2 * sl), ("bank", 2 * sl + 1)]

            def pt2(sl):
                return [("Pt", 2 * sl), ("Pt", 2 * sl + 1)]

            if is_diff:
                kbs = list(range(nkb))
                n = len(kbs)
                for j in range(n + 1):
                    if j < n:
                        kb = kbs[j]
                        c0 = c0_of(kb)
                        sl = j % 2
                        qk_pair(sl, kb)
                        op("act", "activation", out=Pt_all[:, 2 * sl:2 * sl + 2, c0:512],
                           in_=ps_all[:, 2 * sl:2 * sl + 2, c0:512], func=AF.Exp, reads=bk2(sl), writes=pt2(sl))
                    if j >= 1:
                        kb = kbs[j - 1]
                        c0 = c0_of(kb)
                        sl = (j - 1) % 2
                        last = kb == nkb - 1
                        for r in range(2):
                            p = 2 * sl + r
                            mm(bank[4 + r][:, c0:512], vblk(kb), Pt[p][:, c0:512], kb == 0, last,
                               reads=[("Pt", p), ("RB", "kv", s)], writes=[("bank", 4 + r)])
                            mm(bank[6 + r][:, c0:512], CM(C_ONE), Pt[p][:, c0:512], kb == 0, last,
                               reads=[("Pt", p), ("cm",)], writes=[("bank", 6 + r)])
                    if j == 2:
                        run_deferred()
                op("dve", "tensor_copy", out=tf[0][:, :], in_=bank[4][:, :], reads=[("bank", 4)], writes=[("tf", 0)])
                op("act", "copy", out=tf[1][:, :], in_=bank[5][:, :], reads=[("bank", 5)], writes=[("tf", 1)])
                op("dve", "tensor_copy", out=tf[2][:, :], in_=bank[6][:, :], reads=[("bank", 6)], writes=[("tf", 2)])
                op("act", "copy", out=tf[3][:, :], in_=bank[7][:, :], reads=[("bank", 7)], writes=[("tf", 3)])
                ones_i, gvec, gkey = C_ONES128, gdiffs[:, l:l + 1], ("gdiffs", l)
                combine = True
            else:
                kbs = list(range(nkb - 1, -1, -1))
                n = len(kbs)
                run_deferred()
                op("dve", "memset", Ls_all[:, :, :], 0.0, writes=[("Ls", 0), ("Ls", 1)])
                for j in range(n + 2):
                    if j < n:
                        kb = kbs[j]
                        c0 = c0_of(kb)
                        sl = j % 3
                        lsl = j % 2
                        qk_pair(sl, kb, stop=False)
                        op("act", "activation", out=tf_all[:, 2:4, c0:512], in_=ps_all[:, 2 * sl:2 * sl + 2, c0:512],
                           func=AF.Exp, reads=bk2(sl), writes=[("tf", 2), ("tf", 3)])
                        op("act", "activation", out=Lt_all[:, 2 * lsl:2 * lsl + 2, c0:512],
                           in_=tf_all[:, 2:4, c0:512], func=AF.Ln, bias=onecol[:, 0:1], scale=1.0,
                           reads=[("tf", 2), ("tf", 3), ("onecol",)], writes=[("Lt", 2 * lsl), ("Lt", 2 * lsl + 1)])
                    if 1 <= j <= n:
                        kb = kbs[j - 1]
                        c0 = c0_of(kb)
                        sl = (j - 1) % 3
                        lsl = (j - 1) % 2
                        first = kb == nkb - 1
                        for r in range(2):
                            mm(bank[2 * sl + r][:, c0:512], CM(C_NEGTRI), Lt[2 * lsl + r][:, c0:512], False, first,
                               reads=[("Lt", 2 * lsl + r), ("cm",)], writes=[("bank", 2 * sl + r)])
                        if not first:
                            for r in range(2):
                                mm(bank[2 * sl + r][:, c0:512], CM(C_NEGONES), Ls[r][:, c0:512], False, True,
                                   reads=[("Ls", r), ("cm",)], writes=[("bank", 2 * sl + r)])
                        op("dve", "tensor_tensor", out=Ls_all[:, :, c0:512], in0=Ls_all[:, :, c0:512],
                           in1=Lt_all[:, 2 * lsl:2 * lsl + 2, c0:512], op=ALU.add,
                           reads=[("Ls", 0), ("Ls", 1), ("Lt", 2 * lsl), ("Lt", 2 * lsl + 1)],
                           writes=[("Ls", 0), ("Ls", 1)])
                        op("act", "activation", out=Pt_all[:, 2 * lsl:2 * lsl + 2, c0:512],
                           in_=ps_all[:, 2 * sl:2 * sl + 2, c0:512], func=AF.Exp, reads=bk2(sl), writes=pt2(lsl))
                    if j >= 2:
                        kb = kbs[j - 2]
                        c0 = c0_of(kb)
                        lsl = (j - 2) % 2
                        for r in range(2):
                            p = 2 * lsl + r
                            mm(bank[6 + r][:, c0:512], vblk(kb), Pt[p][:, c0:512], kb == nkb - 1, kb == 0,
                               reads=[("Pt", p), ("RB", "kv", s)], writes=[("bank", 6 + r)])
                    if j == 3:
                        run_deferred()
                op("dve", "tensor_copy", out=tf[0][0:64, :], in_=bank[6][0:64, :], reads=[("bank", 6)],
                   writes=[("tf", 0, 0)])
                op("dve", "tensor_copy", out=tf[0][64:128, :], in_=bank[7][64:128, :], reads=[("bank", 7)],
                   writes=[("tf", 0, 1)])
                ones_i, gvec, gkey = C_ONES64B, V("gsb", l), ("vecs",)
                combine = False
            run_deferred()

            def part_b(combine=combine, ones_i=ones_i, gvec=gvec, gkey=gkey, mch=mch, gs=gs, g=g):
                if combine:
                    op("dve", "reciprocal", out=tf[2][:, :], in_=tf[2][:, :], reads=[("tf", 2)], writes=[("tf", 2)])
                    op("dve", "tensor_tensor", out=tf[0][:, :], in0=tf[0][:, :], in1=tf[2][:, :], op=ALU.mult,
                       reads=[("tf", 0), ("tf", 2)], writes=[("tf", 0)])
                    op("dve", "reciprocal", out=tf[3][:, :], in_=tf[3][:, :], reads=[("tf", 3)], writes=[("tf", 3)])
                    op("dve", "tensor_tensor", out=tf[1][:, :], in0=tf[1][:, :], in1=tf[3][:, :], op=ALU.mult,
                       reads=[("tf", 1), ("tf", 3)], writes=[("tf", 1)])
                    op("dve", "scalar_tensor_tensor", out=tf[0][:, :], in0=tf[1][:, :], scalar=neglam[:, l:l + 1],
                       in1=tf[0][:, :], op0=ALU.mult, op1=ALU.add,
                       reads=[("tf", 0), ("tf", 1), ("neglam", l)], writes=[("tf", 0)])
                op("act", "activation", out=sqb[0][:, :], in_=tf[0][:, :], func=AF.Square,
                   reads=[("tf", 0)], writes=[("sqb", 0)])
                mm(bank[0][:, :], CM(ones_i), sqb[0][:, :], True, True, reads=[("sqb", 0), ("cm",)],
                   writes=[("bank", 0)])
                op("act", "activation", out=tf[1][:, :], in_=bank[0][:, :], func=AF.Ln, bias=epscol[:, 0:1],
                   scale=1.0, reads=[("bank", 0), ("epscol",)], writes=[("tf", 1)])
                op("act", "activation", out=tf[1][:, :], in_=tf[1][:, :], func=AF.Exp, scale=-0.5,
                   reads=[("tf", 1)], writes=[("tf", 1)])
                op("dve", "scalar_tensor_tensor", out=mixT[:, mch, gs], in0=tf[0][:, :], scalar=gvec,
                   in1=tf[1][:, :], op0=ALU.mult, op1=ALU.mult, reads=[("tf", 0), ("tf", 1), gkey],
                   writes=[("RA", mch, g)])

            deferred.append(part_b)
        run_deferred()

        for dc in range(8):
            s = load_wA(w_out_d[l * 8 + dc])
            for g in range(NG):
                gs = slice(g * 512, (g + 1) * 512)
                b = nxt("mmb", 6)
                for ec in range(8):
                    mm(bank[b][:, :], wA[s][:, ec, :], mixT[:, ec, gs], ec == 0, ec == 7,
                       reads=[("RA", ec, g), ("wA", s)], writes=[("bank", b)])
                op("dve", "tensor_tensor", out=xT[:, dc, gs], in0=bank[b][:, :], in1=xT[:, dc, gs], op=ALU.add,
                   reads=[("bank", b), ("xT", dc, g)], writes=[("xT", dc, g)])

        op("dve", "memset", onecol[:, :], 1.0, reads=[("RA",), ("RB",), ("qT",)], writes=[("RA",), ("RB",), ("qT",), ("onecol",)])
        for g in range(NG):
            gs = slice(g * 512, (g + 1) * 512)
            nb = 6 + nxt("nb", 2)
            t = rmsnorm_stats(lambda kc: xT[:, kc, gs], lambda kc: ("xT", kc, g), g, nb)
            for kc in range(8):
                op("dve", "scalar_tensor_tensor", out=h2T[:, kc, gs], in0=xT[:, kc, gs],
                   scalar=V("g2", l * 8 + kc), in1=tf[t][:, :], op0=ALU.mult, op1=ALU.mult,
                   reads=[("xT", kc, g), ("tf", t), ("vecs",)], writes=[("RA", "h2", kc, g)])
        for fh in range(2):
            for fl in range(NFH):
                fc = fh * NFH + fl
                sg = load_wA(w_gate_d[l * NFC + fc])
                su = load_wA(w_up_d[l * NFC + fc])
                for g in range(NG):
                    gs = slice(g * 512, (g + 1) * 512)
                    bg = nxt("mmb", 6)
                    for kc in range(8):
                        mm(bank[bg][:, :], wA[sg][:, kc, :], h2T[:, kc, gs], kc == 0, kc == 7,
                           reads=[("RA", "h2", kc, g), ("wA", sg)], writes=[("bank", bg)])
                    bu = nxt("mmb", 6)
                    for kc in range(8):
                        mm(bank[bu][:, :], wA[su][:, kc, :], h2T[:, kc, gs], kc == 0, kc == 7,
                           reads=[("RA", "h2", kc, g), ("wA", su)], writes=[("bank", bu)])
                    t = nxt("tf", 4)
                    op("act", "activation", out=tf[t][:, :], in_=bank[bg][:, :], func=AF.Silu,
                       reads=[("bank", bg)], writes=[("tf", t)])
                    op("dve", "tensor_tensor", out=actT(fl)[:, gs], in0=bank[bu][:, :], in1=tf[t][:, :], op=ALU.mult,
                       reads=[("bank", bu), ("tf", t)], writes=[actkey(fl, g)])
            for dc in range(8):
                sd = nxt("dn", 2)
                op("pool", "dma_start", out=wDn[sd][:, :, :],
                   in_=w_down_d[l * 8 + dc][:, fh * NFH:(fh + 1) * NFH, :],
                   writes=[("qT", "dn", sd)], chan=f"wD{sd}")
                for g in range(NG):
                    gs = slice(g * 512, (g + 1) * 512)
                    b = nxt("mmb", 6)
                    for fl in range(NFH):
                        mm(bank[b][:, :], wDn[sd][:, fl, :], actT(fl)[:, gs], fl == 0, fl == NFH - 1,
                           reads=[actkey(fl, g), ("qT", "dn", sd)], writes=[("bank", b)])
                    op("dve", "tensor_tensor", out=xT[:, dc, gs], in0=bank[b][:, :], in1=xT[:, dc, gs], op=ALU.add,
                       reads=[("bank", b), ("xT", dc, g)], writes=[("xT", dc, g)])
        op("dve", "memset", onecol[:, :], 1.0, reads=[("RA",), ("RB",), ("qT",)], writes=[("RA",), ("RB",), ("qT",), ("onecol",)])

    for g in range(NG):
        gs = slice(g * 512, (g + 1) * 512)
        nb = 6 + nxt("nb", 2)
        t = rmsnorm_stats(lambda kc: xT[:, kc, gs], lambda kc: ("xT", kc, g), g, nb, tslot=g % 2)
        for kc in range(8):
            o = nxt("ot", 2)
            op("dve", "scalar_tensor_tensor", out=ot[o][:, :], in0=xT[:, kc, gs],
               scalar=V("gf", kc), in1=tf[t][:, :], op0=ALU.mult, op1=ALU.mult,
               reads=[("xT", kc, g), ("tf", t), ("vecs",)], writes=[("tf", 2 + o)])
            op("sp", "dma_start", out=outT_d[kc * 128:(kc + 1) * 128, gs], in_=ot[o][:, :],
               reads=[("tf", 2 + o)], writes=[("out", kc, g)], chan=f"out{o}")

    print("sbuf bytes remaining per partition:", nc.sbuf_bytes_remaining)
    chans = list(T.chan_ops.keys())
    sem_guards = []

    def mksem(name):
        g_ = nc.semaphore(name)
        sem_guards.append(g_)
        return g_.__enter__()

    esem = {e: mksem(f"e_{e}") for e in ("pe", "act", "dve", "pool")}
    csem = {c: mksem(f"c_{c}") for c in chans}
    with nc.Block() as block:
        @block.sync
        def _(e):
            T.emit("sp", e, esem, csem)
            T.final_waits(e, esem, csem)

        @block.tensor
        def _(e):
            T.emit("pe", e, esem, csem)

        @block.scalar
        def _(e):
            T.emit("act", e, esem, csem)

        @block.vector
        def _(e):
            T.emit("dve", e, esem, csem)

        @block.gpsimd
        def _(e):
            T.emit("pool", e, esem, csem)
    for g_ in reversed(sem_guards):
        g_.__exit__(None, None, None)
    for g_ in reversed(guards):
        g_.__exit__(None, None, None)
    return nc, T


def _const_mats(j):
    bf = ml_dtypes.bfloat16
    k = np.arange(128)[:, None]
    q = np.arange(128)[None, :]
    m = np.zeros((NCM, 128, 128), np.float32)
    m[C_ONESD] = 1.0 / D
    m[C_ONES128] = 1.0 / 128
    m[C_ONES64B][:64, :64] = 1.0 / 64
    m[C_ONES64B][64:, 64:] = 1.0 / 64
    m[C_ONE] = 1.0
    m[C_IDENT] = np.eye(128)
    m[C_NEGTRI] = np.where(k >= q, -1.0, 0.0)
    m[C_NEGONES] = -1.0
    diag_d = np.where(k <= q, 0.0, NEG)
    diag_s = np.where(k < q, 0.0, NEG)
    full = np.zeros((128, 128))
    none = np.full((128, 128), NEG)
    if j == 0:
        m[C_MA_D], m[C_MB_D], m[C_MA_S], m[C_MB_S] = diag_d, none, diag_s, none
    else:
        m[C_MA_D], m[C_MB_D], m[C_MA_S], m[C_MB_S] = full, diag_d, full, diag_s
    return np.ascontiguousarray(m.transpose(1, 0, 2).reshape(128, NCM * 128)).astype(bf)


def _pack_vecs(depth, j, norm1_g, norm2_g, final_norm_g, pool_scale, diff_norm_g, sb_norm_g,
               lam_q1, lam_k1, lam_q2, lam_k2):
    voff, NV = vec_layout(depth)
    v = np.zeros((128, NV), np.float32)
    p = np.arange(128)
    for l in range(depth):
        for kc in range(8):
            v[:, voff["g1"] + l * 8 + kc] = norm1_g[l, kc * 128:(kc + 1) * 128]
            v[:, voff["g2"] + l * 8 + kc] = norm2_g[l, kc * 128:(kc + 1) * 128]
        for c in range(2):
            v[:, voff["pscale"] + l * 2 + c] = pool_scale[l, c * 128:(c + 1) * 128]
        v[:, voff["gdiff"] + l] = diff_norm_g[l]
        v[:, voff["gsb"] + l] = sb_norm_g[l][p % 64]
        for i, a in enumerate((lam_q1, lam_k1, lam_q2, lam_k2)):
            o = voff["lam"] + (l * 4 + i) * 64
            v[:, o:o + 64] = a[l][None, :]
    for kc in range(8):
        v[:, voff["gf"] + kc] = final_norm_g[kc * 128:(kc + 1) * 128]
    for c in range(2):
        w = np.array([POOL_WINDOWS[2 * c + pp // 64] for pp in p], np.float32)
        v[:, voff["invw"] + c] = 1.0 / w
        for t in range(16):
            cnt = np.minimum(t + 1, w) if j == 0 else w
            v[:, voff["coef"] + c * 16 + t] = 1.0 / cnt
    v[:, voff["sel"] + 0] = 0.0 if j == 0 else 1.0
    v[:, voff["sel"] + 1] = 1.0 if j == 0 else 0.0
    return v


def _pack_poolw(depth, pool_w):
    m = np.zeros((128, depth * 2, 128), np.float32)
    for l in range(depth):
        for c in range(2):
            m[0:64, l * 2 + c, 0:64] = pool_w[l, 2 * c]
            m[64:128, l * 2 + c, 64:128] = pool_w[l, 2 * c + 1]
    return np.ascontiguousarray(m.reshape(128, depth * 2 * 128))


_CACHE = {}


def run(x, norm1_g, w_in, pool_w, pool_scale, lam_q1, lam_k1, lam_q2, lam_k2, diff_norm_g, sb_norm_g,
        w_out, norm2_g, w_gate, w_up, w_down, final_norm_g, trace=False):
    f = lambda a: np.ascontiguousarray(np.asarray(a, dtype=np.float32))
    x = f(x)
    B, S, _ = x.shape
    depth = int(np.asarray(w_in).shape[0])
    assert B == 4 and S % 1024 == 0
    NB = S // 256
    TL = NB * 128
    key = (S, depth)
    if key not in _CACHE:
        _CACHE[key] = build(S, depth)[0]
    nc = _CACHE[key]
    def tile_w(w, kchunks, nchunks):
        w = f(w).reshape(depth, kchunks, 128, nchunks, 128).transpose(0, 3, 2, 1, 4)
        return np.ascontiguousarray(w).reshape(depth * nchunks, 128, kchunks, 128)

    w_in, w_out = tile_w(w_in, 8, 20), tile_w(w_out, 8, 8)
    w_gate, w_up, w_down = tile_w(w_gate, 8, NFC), tile_w(w_up, 8, NFC), tile_w(w_down, NFC, 8)
    pw = _pack_poolw(depth, f(pool_w))
    in_maps = []
    for c in range(8):
        b, j = c // 2, c % 2
        xs = x[b].reshape(NB, 2, 128, D)[:, j].reshape(TL, D)
        in_maps.append({
            "xT": np.ascontiguousarray(xs.T),
            "w_in": w_in, "w_out": w_out, "w_gate": w_gate, "w_up": w_up, "w_down": w_down,
            "vecs": _pack_vecs(depth, j, f(norm1_g), f(norm2_g), f(final_norm_g), f(pool_scale),
                               f(diff_norm_g), f(sb_norm_g), f(lam_q1), f(lam_k1), f(lam_q2), f(lam_k2)),
            "poolw": pw,
            "cmats": _const_mats(j),
        })
    res = run_bass_kernel_spmd(nc, in_maps, core_ids=list(range(8)), **({"trace": True} if trace else {}))
    out = np.empty((B, S, D), np.float32)
    for c in range(8):
        b, j = c // 2, c % 2
        o = np.asarray(res.results[c]["outT"], dtype=np.float32).T.reshape(NB, 128, D)
        out[b].reshape(NB, 2, 128, D)[:, j] = o
    return out, res


def kernel(**inputs):
    out, _ = run(**inputs)
    return out
```
